# Optimizing a Trainium2 kernel written in Bass

```python
import math
import jax, jax.numpy as jnp
from jax import lax
import numpy as np

D_MODEL = 1024
BATCH = 16
SEQ = 256
DEPTH = 4
DEC_BATCH = 8
DEC_SEQ = 2048
PAST_LEN = 512

GRID_W = 64
MIX_W = D_MODEL
POOL_W = D_MODEL // 4
POOL_GROUPS = 4
POOL_WINDOWS = (2, 4, 8, 16)
POOL_GW = POOL_W // POOL_GROUPS
CONV_W = D_MODEL // 4
CONV_K = 31
MLA_W = MIX_W - POOL_W - CONV_W
N_HEADS = 4
V_DIM = MLA_W // N_HEADS
NOPE_DIM = 128
ROPE_DIM = 64
Q_LORA = 384
KV_LORA = 256
IN_W = POOL_W + Q_LORA + KV_LORA + ROPE_DIM + 2 * CONV_W
D_FF = 2816
FFN_K = 3
ROPE_THETA = 10000.0
Q_BLOCK = 128
EPS = 1e-6

kernel_name = 'hybrid_pool_mla_conformer_diffusion_step'


def rmsnorm(x, g):
    xf = x.astype(jnp.float32)
    y = xf * lax.rsqrt(jnp.mean(xf * xf, axis=-1, keepdims=True) + EPS)
    return (y * g.astype(jnp.float32)).astype(x.dtype)


def modulate(h, shift, scale):
    return h * (1 + scale) + shift


def dwconv(x, w, b):
    K, C = w.shape
    y = lax.conv_general_dilated(x, w[:, None, :].astype(x.dtype), window_strides=(1,),
                                 padding=[(K // 2, K // 2)],
                                 dimension_numbers=('NWC', 'WIO', 'NWC'),
                                 feature_group_count=C)
    return y + b


def pool_mixer(z, w, scale):
    B, L, _ = z.shape
    zf = z.astype(jnp.float32).reshape(B, L, POOL_GROUPS, POOL_GW)
    cs = jnp.concatenate([jnp.zeros((B, 1, POOL_GROUPS, POOL_GW), jnp.float32),
                          jnp.cumsum(zf, axis=1)], axis=1)
    t = jnp.arange(L)
    outs = []
    for g, win in enumerate(POOL_WINDOWS):
        start = jnp.clip(t - win // 2, 0, L)
        end = jnp.clip(t - win // 2 + win, 0, L)
        s = cs[:, end, g] - cs[:, start, g]
        outs.append(s / (end - start).astype(jnp.float32)[None, :, None])
    pooled = jnp.stack(outs, axis=2)
    d = (pooled - zf).astype(z.dtype)
    y = jnp.einsum('blgc,gcd->blgd', d, w).reshape(B, L, POOL_W)
    return y * scale


def rope_angles(L):
    rows = L // GRID_W
    row = jnp.repeat(jnp.arange(rows, dtype=jnp.float32), GRID_W)
    col = jnp.tile(jnp.arange(GRID_W, dtype=jnp.float32), rows)
    n = ROPE_DIM // 4
    inv = ROPE_THETA ** (-jnp.arange(n, dtype=jnp.float32) / n)
    return row[:, None] * inv, col[:, None] * inv


def rotate(x, ang):
    n = ang.shape[-1]
    x1, x2 = x[..., :n], x[..., n:]
    cos = jnp.cos(ang).astype(x.dtype)
    sin = jnp.sin(ang).astype(x.dtype)
    return jnp.concatenate([x1 * cos - x2 * sin, x1 * sin + x2 * cos], axis=-1)


def axial_rope(x, ang_r, ang_c):
    half = ROPE_DIM // 2
    return jnp.concatenate([rotate(x[..., :half], ang_r), rotate(x[..., half:], ang_c)], axis=-1)


def mla_attend(q_nope, q_rope, k_nope, k_rope, v):
    B, Lq, H, _ = q_nope.shape
    nb = Lq // Q_BLOCK
    scale = 1.0 / math.sqrt(NOPE_DIM + ROPE_DIM)

    def to_blocks(a):
        return a.reshape(B, nb, Q_BLOCK, *a.shape[2:]).swapaxes(0, 1)

    def block(qs):
        qn, qr = qs
        s = (jnp.einsum('bqhd,bkhd->bhqk', qn, k_nope).astype(jnp.float32)
             + jnp.einsum('bqhr,bkr->bhqk', qr, k_rope).astype(jnp.float32))
        p = jax.nn.softmax(s * scale, axis=-1).astype(v.dtype)
        return jnp.einsum('bhqk,bkhd->bqhd', p, v)

    o = lax.map(block, (to_blocks(q_nope), to_blocks(q_rope)))
    return o.swapaxes(0, 1).reshape(B, Lq, H * V_DIM)


def conformer_conv(z, w, b, g, beta):
    a, gate = jnp.split(z, 2, axis=-1)
    u = dwconv(a * jax.nn.sigmoid(gate), w, b)
    uf = u.astype(jnp.float32)
    mu = jnp.mean(uf, axis=-1, keepdims=True)
    var = jnp.mean(jnp.square(uf - mu), axis=-1, keepdims=True)
    un = (uf - mu) * lax.rsqrt(var + EPS) * g.astype(jnp.float32) + beta.astype(jnp.float32)
    return jax.nn.silu(un).astype(z.dtype)


def conv_ffn(h, w_up, cw, cb, w_down):
    u = dwconv(h @ w_up, cw, cb)
    g, v = jnp.split(u, 2, axis=-1)
    return (jax.nn.silu(g) * v) @ w_down


def setup_inputs(seed: int = 0) -> dict:
    key = jax.random.key(seed)
    ks = jax.random.split(key, 32)

    def nrm(k, shape, s):
        return jax.random.normal(k, shape, jnp.float32) * s

    D = D_MODEL
    return {
        'x_prompt': nrm(ks[0], (BATCH, SEQ, D), 1.0),
        'x_sample': nrm(ks[1], (DEC_BATCH, DEC_SEQ, D), 1.0),
        'cache_ckv': nrm(ks[2], (DEC_BATCH, DEPTH, PAST_LEN, KV_LORA), 1.0),
        'cache_krope': nrm(ks[3], (DEC_BATCH, DEPTH, PAST_LEN, ROPE_DIM), 1.0),
        'c': nrm(ks[4], (DEC_BATCH, D), 1.0),
        'c_ctx': nrm(ks[5], (D,), 1.0),
        'ada_w': nrm(ks[6], (DEPTH, D, 6 * D), 0.5 * D ** -0.5),
        'ada_b': nrm(ks[7], (DEPTH, 6 * D), 0.02),
        'norm1_g': 1.0 + nrm(ks[8], (DEPTH, D), 0.05),
        'w_in': nrm(ks[9], (DEPTH, D, IN_W), D ** -0.5),
        'pool_w': nrm(ks[10], (DEPTH, POOL_GROUPS, POOL_GW, POOL_GW), POOL_GW ** -0.5),
        'pool_scale': 1.0 + nrm(ks[11], (DEPTH, POOL_W), 0.1),
        'q_norm_g': 1.0 + nrm(ks[12], (DEPTH, Q_LORA), 0.05),
        'w_uq': nrm(ks[13], (DEPTH, Q_LORA, N_HEADS * (NOPE_DIM + ROPE_DIM)), Q_LORA ** -0.5),
        'kv_norm_g': 1.0 + nrm(ks[14], (DEPTH, KV_LORA), 0.05),
        'w_ukv': nrm(ks[15], (DEPTH, KV_LORA, N_HEADS * (NOPE_DIM + V_DIM)), KV_LORA ** -0.5),
        'conv_w': nrm(ks[16], (DEPTH, CONV_K, CONV_W), CONV_K ** -0.5),
        'conv_b': nrm(ks[17], (DEPTH, CONV_W), 0.02),
        'conv_ln_g': 1.0 + nrm(ks[18], (DEPTH, CONV_W), 0.05),
        'conv_ln_b': nrm(ks[19], (DEPTH, CONV_W), 0.02),
        'w_out': nrm(ks[20], (DEPTH, MIX_W, D), MIX_W ** -0.5),
        'norm2_g': 1.0 + nrm(ks[21], (DEPTH, D), 0.05),
        'w_up': nrm(ks[22], (DEPTH, D, 2 * D_FF), D ** -0.5),
        'ffn_conv_w': nrm(ks[23], (DEPTH, FFN_K, 2 * D_FF), FFN_K ** -0.5),
        'ffn_conv_b': nrm(ks[24], (DEPTH, 2 * D_FF), 0.02),
        'w_down': nrm(ks[25], (DEPTH, D_FF, D), D_FF ** -0.5),
        'final_g': 1.0 + nrm(ks[26], (D,), 0.05),
    }


def reference(x_prompt, x_sample, cache_ckv, cache_krope, c, c_ctx, ada_w, ada_b, norm1_g, w_in,
              pool_w, pool_scale, q_norm_g, w_uq, kv_norm_g, w_ukv, conv_w, conv_b, conv_ln_g,
              conv_ln_b, w_out, norm2_g, w_up, ffn_conv_w, ffn_conv_b, w_down, final_g):

    def layer(x, mod, l, ang, ctx):
        B, L, _ = x.shape
        sh1, sc1, g1, sh2, sc2, g2 = jnp.split(mod, 6, axis=-1)
        h = modulate(rmsnorm(x, norm1_g[l]), sh1, sc1)
        z = h @ w_in[l]
        o1 = POOL_W
        o2 = o1 + Q_LORA
        o3 = o2 + KV_LORA
        o4 = o3 + ROPE_DIM
        zp, zq, zkv, zkr, zc = z[..., :o1], z[..., o1:o2], z[..., o2:o3], z[..., o3:o4], z[..., o4:]
        y_pool = pool_mixer(zp, pool_w[l], pool_scale[l])
        q = (rmsnorm(zq, q_norm_g[l]) @ w_uq[l]).reshape(B, L, N_HEADS, NOPE_DIM + ROPE_DIM)
        q_nope, q_rope = q[..., :NOPE_DIM], q[..., NOPE_DIM:]
        ckv = rmsnorm(zkv, kv_norm_g[l])
        kv = (ckv @ w_ukv[l]).reshape(B, L, N_HEADS, NOPE_DIM + V_DIM)
        k_nope, v = kv[..., :NOPE_DIM], kv[..., NOPE_DIM:]
        k_rope = zkr
        if ang is not None:
            ang_r, ang_c = ang
            q_rope = axial_rope(q_rope, ang_r[:, None, :], ang_c[:, None, :])
            k_rope = axial_rope(k_rope, ang_r, ang_c)
        if ctx is not None:
            ctx_ckv, ctx_kr = ctx
            Lc = ctx_ckv.shape[1]
            kv_c = (ctx_ckv @ w_ukv[l]).reshape(B, Lc, N_HEADS, NOPE_DIM + V_DIM)
            k_nope = jnp.concatenate([k_nope, kv_c[..., :NOPE_DIM]], axis=1)
            v = jnp.concatenate([v, kv_c[..., NOPE_DIM:]], axis=1)
            k_rope = jnp.concatenate([k_rope, ctx_kr], axis=1)
        y_att = mla_attend(q_nope, q_rope, k_nope, k_rope, v)
        y_conv = conformer_conv(zc, conv_w[l], conv_b[l], conv_ln_g[l], conv_ln_b[l])
        y_mix = jnp.concatenate([y_pool, y_att, y_conv], axis=-1) @ w_out[l]
        x = x + g1 * y_mix
        h2 = modulate(rmsnorm(x, norm2_g[l]), sh2, sc2)
        x = x + g2 * conv_ffn(h2, w_up[l], ffn_conv_w[l], ffn_conv_b[l], w_down[l])
        return x, ckv, zkr

    xc = x_prompt
    ckv_list = []
    kr_list = []
    for l in range(DEPTH):
        mod = (jax.nn.silu(c_ctx) @ ada_w[l] + ada_b[l])[None, None, :]
        xc, ckv, kr = layer(xc, mod, l, None, None)
        ckv_list.append(ckv)
        kr_list.append(kr)
    y_prompt = rmsnorm(xc, final_g)
    state_ckv = jnp.stack(ckv_list, axis=1)
    state_krope = jnp.stack(kr_list, axis=1)

    ang = rope_angles(x_sample.shape[1])
    xs = x_sample
    for l in range(DEPTH):
        mod = (jax.nn.silu(c) @ ada_w[l] + ada_b[l])[:, None, :]
        xs, _, _ = layer(xs, mod, l, ang, (cache_ckv[:, l], cache_krope[:, l]))
    y_sample = rmsnorm(xs, final_g)

    return (y_prompt, y_sample, state_ckv, state_krope)
```

```python
import math
from contextlib import ExitStack

import numpy as np
import concourse.bass as bass
import concourse.mybir as mybir
from concourse.bass_utils import run_bass_kernel_spmd

F32 = mybir.dt.float32
BF16 = mybir.dt.bfloat16
U8 = mybir.dt.uint8
AF = mybir.ActivationFunctionType
ALU = mybir.AluOpType

NCORES = 8
D = 1024
DEPTH = 4
LS = 2048
LP = 256
NTOK = LS + 2 * LP
PAST = 512
DFF = 2816
NJ = DFF // 128
EPS = 1e-6
INW = 1536
MODS_INTERLEAVE = True
ATT_SCALE = 1.0 / math.sqrt(192.0)

VO = {}
_o = 0
for _n, _w in (("n1g", 8), ("n2g", 8), ("adab", 48), ("pscale", 2), ("qg", 3), ("kvg", 2),
               ("convw", 62), ("convb", 2), ("lng", 2), ("lnb", 2), ("fcw", 132), ("fcb", 44)):
    VO[_n] = _o
    _o += _w
VPL = _o
NV = VPL * DEPTH + 8

ENGS = ("pe", "act", "dve", "pool", "sp")
NDMA = 8


class Res:
    __slots__ = ("name", "w", "rs", "rd")

    def __init__(self, name):
        self.name = name
        self.w = None
        self.rs = {}
        self.rd = []


class _Op:
    __slots__ = ("eng", "fns", "waits", "sig", "dma", "tok", "pre")

    def __init__(self, eng, fns, dma):
        self.eng = eng
        self.fns = fns
        self.waits = []
        self.sig = False
        self.dma = dma
        self.tok = None
        self.pre = None


class Prog:
    def __init__(self, nc, stack):
        self.nc = nc
        self.ops = []
        self.sems = {e: stack.enter_context(nc.semaphore("s_" + e)) for e in ENGS}
        self.dsems = {e: [stack.enter_context(nc.semaphore(f"d_{e}{i}")) for i in range(NDMA)]
                      for e in ("sp", "pool")}
        self.dcnt = {e: [0] * NDMA for e in self.dsems}
        self.dnext = {e: 0 for e in self.dsems}
        self.cnt = {e: 0 for e in ENGS}
        self.waited = {e: {} for e in ENGS}
        self.last = {e: None for e in ENGS}
        self.dmas = []

    def op(self, eng, fns, reads=(), writes=(), dma=False):
        if callable(fns) or isinstance(fns, tuple):
            fns = [fns]
        fns = [(lambda e, m=f[0], kw=f[1]: getattr(e, m)(**kw)) if isinstance(f, tuple) else f for f in fns]
        o = _Op(eng, list(fns), dma)
        o.tok = ["pending", None, o]

        def _flat(xs):
            out = []
            for x in xs:
                if isinstance(x, (list, tuple)):
                    out.extend(x)
                else:
                    out.append(x)
            return out
        reads = _flat(reads)
        writes = _flat(writes)
        toks = []
        for r in reads:
            toks.append(r.w)
        for r in writes:
            toks.append(r.w)
            toks.extend(r.rs.values())
            toks.extend(r.rd)
        for t in toks:
            if t is not None and t[2] is not o:
                if eng == "pe" and t[2].eng == "pe" and not t[2].dma:
                    continue
                o.waits.append(t)
                t[2].sig = True
        for r in reads:
            if dma:
                r.rd.append(o.tok)
            else:
                r.rs[eng] = o.tok
        for r in writes:
            r.w = o.tok
            r.rs = {}
            r.rd = []
        self.ops.append(o)
        if dma:
            self.dmas.append(o.tok)
        else:
            self.last[eng] = o
        return o

    def barrier(self):
        toks = []
        for e in ENGS:
            lo = self.last[e]
            if lo is not None:
                lo.sig = True
                toks.append(lo.tok)
        toks.extend(self.dmas)
        self.dmas = []
        for e in ENGS:
            o = _Op(e, [], False)
            o.tok = ["pending", None, o]
            o.waits = list(toks)
            self.ops.append(o)

    def emit(self, final_waits=()):
        nc = self.nc
        for o in self.ops:
            if o.dma:
                q = o.eng
                i = self.dnext[q]
                self.dnext[q] = (i + 1) % NDMA
                prev = self.dcnt[q][i]
                o.pre = (("d", q, i), prev) if prev > 0 else None
                self.dcnt[q][i] += 16
                o.tok[0] = ("d", q, i)
                o.tok[1] = self.dcnt[q][i]
                o.sig = True
            elif o.sig:
                self.cnt[o.eng] += 1
                o.tok[0] = ("e", o.eng)
                o.tok[1] = self.cnt[o.eng]
        per = {e: [] for e in ENGS}
        for o in self.ops:
            per[o.eng].append(o)

        def semof(key):
            if key[0] == "e":
                return self.sems[key[1]]
            return self.dsems[key[1]][key[2]]

        def run(e, handle):
            waited = self.waited[e]
            for o in per[e]:
                need = {}
                if o.pre is not None:
                    need[o.pre[0]] = o.pre[1]
                for t in o.waits:
                    k, v = t[0], t[1]
                    if need.get(k, 0) < v:
                        need[k] = v
                for k, v in need.items():
                    if waited.get(k, 0) < v:
                        handle.wait_ge(semof(k), v)
                        waited[k] = v
                ins = None
                for f in o.fns:
                    ins = f(handle)
                if o.sig and ins is not None:
                    if o.dma:
                        ins.then_inc(semof(o.tok[0]), 16)
                    else:
                        ins.then_inc(self.sems[e], 1)
            if e == "sp":
                for t in final_waits:
                    k, v = t[0], t[1]
                    if waited.get(k, 0) < v:
                        handle.wait_ge(semof(k), v)
                        waited[k] = v

        with nc.Block() as block:
            @block.tensor
            def _(h):
                run("pe", h)

            @block.scalar
            def _(h):
                run("act", h)

            @block.vector
            def _(h):
                run("dve", h)

            @block.gpsimd
            def _(h):
                run("pool", h)

            @block.sync
            def _(h):
                run("sp", h)


class Arena:
    def __init__(self, ap, size):
        self.ap = ap
        self.size = size
        self.off = 0

    def alloc(self, shape, dt, part=128):
        esz = 4 if dt == F32 else 2
        n = 1
        for s in shape:
            n *= s
        nb = (n * esz + 63) // 64 * 64
        assert self.off + nb <= self.size, f"arena overflow {self.off}+{nb}>{self.size}"
        v = self.ap[:, self.off:self.off + n * esz].bitcast(dt)
        self.off += nb
        if len(shape) == 2:
            v = v.rearrange("p (a b) -> p a b", b=shape[1])
        elif len(shape) == 3:
            v = v.rearrange("p (a b c) -> p a b c", b=shape[1], c=shape[2])
        return v


def build(depth=DEPTH):
    nc = bass.Bass("TRN2", target_bir_lowering=False, dynamic_dma_scratch_size=4096)

    def din(name, shape):
        return nc.dram_tensor(name, list(shape), F32, kind="ExternalInput").ap()

    xin = din("xin", (8, 128, NTOK))
    vecs_d = din("vecs", (128, NV))
    cs_d = din("cs", (128, LS))
    ident_d = din("ident", (128, 128))
    craw_d = din("craw", (128, 8, 2))
    poolf_d = din("poolf", (128, 2, 2, 8))
    invw_d = din("invw", (128, 2))
    pwbd_d = din("pwbd", (DEPTH, 128, 2, 128))
    ada_d = din("ada", (DEPTH, 12, 128, 8, 512))
    win_d = din("win", (DEPTH, 128, 8, INW))
    wuq_d = din("wuq", (DEPTH, 128, 3, 1024))
    wukv_d = din("wukv", (DEPTH, 128, 2, 1024))
    wout_d = din("wout", (DEPTH, 128, 8, 1024))
    wup_d = din("wup", (DEPTH, NJ, 128, 8, 256))
    wdn_d = din("wdn", (DEPTH, 8, 128, NJ, 128))
    cckv_d = din("cckv", (DEPTH, 128, 2, PAST))
    ckr_d = din("ckr", (DEPTH, 64, PAST))
    cdiag_d = din("cdiag", (DEPTH, 128, 62, 128))

    y_d = nc.dram_tensor("y", [8, 128, NTOK], F32, kind="ExternalOutput").ap()
    sckv_d = nc.dram_tensor("sckv", [DEPTH, 2, 2, 128, LP], F32, kind="ExternalOutput").ap()
    skr_d = nc.dram_tensor("skr", [DEPTH, 2, 64, LP], F32, kind="ExternalOutput").ap()
    X0 = nc.dram_tensor("xs0", [8, 128, NTOK], F32).ap()
    X1 = nc.dram_tensor("xs1", [8, 128, NTOK], F32).ap()

    def xview(t):
        return t.rearrange("k p t -> p k t")

    with ExitStack() as st:
        ASZ = 188 * 1024
        arena_t = st.enter_context(nc.sbuf_tensor("arena", [128, ASZ], U8))
        AR = Arena(arena_t, ASZ)
        psb = [st.enter_context(nc.psum_tensor(f"ps{i}", [128, 512], F32)) for i in range(8)]
        Rph = [Res(f"ph{i}") for i in range(16)]
        Rps = [[Rph[2 * i], Rph[2 * i + 1]] for i in range(8)]
        hrot = [0]

        def nbh():
            b = nb()
            return psb[b][:, 0:256], Rps[b], b
        P = Prog(nc, st)
        rot = [0]
        reserved = set()

        def nb():
            while True:
                i = rot[0]
                rot[0] = (i + 1) % 8
                if i not in reserved:
                    return i

        vecs = AR.alloc([NV], F32)
        cs = AR.alloc([LS], F32)
        ident_f = AR.alloc([128], F32)
        ones_f = AR.alloc([128], F32)
        ones_b = AR.alloc([128], BF16)
        craw = AR.alloc([8, 2], F32)
        csil = AR.alloc([8, 2], F32)
        MOD = AR.alloc([DEPTH, 48, 2], F32)
        AMOD = AR.alloc([DEPTH * 2, 8, 2], F32)
        poolf = AR.alloc([2, 2, 8], F32)
        invw = AR.alloc([2], F32)
        pwbd = AR.alloc([2, 128], BF16)
        R_const = Res("const")
        R_mods = [Res(f"mod{i}") for i in range(DEPTH)]
        cur_l = [0]

        def Rm():
            return R_mods[cur_l[0]]
        R_pwbd = Res("pwbd")
        const_end = AR.off

        P.op("sp", ("dma_start", dict(out=vecs, in_=vecs_d)), writes=[R_const], dma=True)
        P.op("sp", ("dma_start", dict(out=cs, in_=cs_d)), writes=[R_const], dma=True)
        P.op("sp", ("dma_start", dict(out=ident_f, in_=ident_d)), writes=[R_const], dma=True)
        P.op("sp", ("dma_start", dict(out=craw, in_=craw_d)), writes=[R_const], dma=True)
        P.op("sp", ("dma_start", dict(out=poolf, in_=poolf_d)), writes=[R_const], dma=True)
        P.op("sp", ("dma_start", dict(out=invw, in_=invw_d)), writes=[R_const], dma=True)
        P.op("dve", ("memset", dict(ap=ones_f, constant=1.0)), writes=[R_const])
        P.op("dve", ("memset", dict(ap=ones_b, constant=1.0)), writes=[R_const])
        P.op("act", ("activation", dict(out=csil, in_=craw, func=AF.Silu)), reads=[R_const], writes=[R_const])

        def vcol(l, name, i, n=1, p0=0, p1=128):
            c = l * VPL + VO[name] + i
            return vecs[p0:p1, c:c + n]

        csil_b = AR.alloc([8, 2], BF16)
        P.op("dve", ("tensor_copy", dict(out=csil_b, in_=csil)), reads=[R_const], writes=[R_const])
        epsc = AR.alloc([2], F32)
        P.op("dve", ("memset", dict(ap=epsc, constant=EPS)), writes=[R_const])
        reg0 = AR.off
        FFN_HI = reg0 + 104 * 1024

        def mod_tasks(l, stg, Rstg, rowb, Rrow, pb):
            pst = psb[pb][:, 0:96].rearrange("p (q g) -> p q g", g=2)
            ns = len(stg)

            def mk_load(blk):
                def f():
                    P.op("pool", ("dma_start", dict(out=stg[blk % ns], in_=ada_d[l, blk], max_dma_last_dim=4096)),
                         writes=[Rstg[blk % ns]], dma=True)
                return f

            def mk_block(blk):
                def f():
                    sidx = blk % ns
                    r = blk % 2
                    pr = nb()
                    fns = [("matmul", dict(out=psb[pr][0:2, :], lhsT=csil_b[:, k, :], rhs=stg[sidx][:, k, :],
                                           start=(k == 0), stop=(k == 7))) for k in range(8)]
                    P.op("pe", fns, reads=[Rstg[sidx], R_const], writes=[Rps[pr]])
                    P.op("act", ("activation", dict(out=rowb[r][0:2, :], in_=psb[pr][0:2, :], func=AF.Copy)),
                         reads=[Rps[pr]], writes=[Rrow[r]])
                return f

            def mk_trans(blk):
                def f():
                    r = blk % 2
                    fns = [("matmul", dict(out=pst[:, blk * 4 + i, :], lhsT=rowb[r][0:2, i * 128:(i + 1) * 128],
                                           rhs=ident_f[0:2, 0:2], start=True, stop=True)) for i in range(4)]
                    P.op("pe", fns, reads=[Rrow[r], R_const], writes=[Rps[pb]])
                return f

            def final():
                for g in range(2):
                    P.op("dve", ("tensor_tensor", dict(
                        out=MOD[:, l, :, g], in0=pst[:, :, g], in1=vcol(l, "adab", 0, 48), op=ALU.add)),
                        reads=[Rps[pb], R_const], writes=[R_mods[l]])
                for which, (sc0, gn) in enumerate(((8, "n1g"), (32, "n2g"))):
                    for g in range(2):
                        P.op("dve", ("scalar_tensor_tensor", dict(
                            out=AMOD[:, l * 2 + which, :, g], in0=MOD[:, l, sc0:sc0 + 8, g], scalar=1.0,
                            in1=vcol(l, gn, 0, 8), op0=ALU.add, op1=ALU.mult)),
                            reads=[R_mods[l], R_const], writes=[R_mods[l]])
            return ([mk_load(b) for b in range(12)], [mk_block(b) for b in range(12)],
                    [mk_trans(b) for b in range(12)], final)

        def emit_mods():
            AR.off = reg0
            stg = [AR.alloc([8, 512], BF16) for _ in range(4)]
            rowb = [AR.alloc([512], F32) for _ in range(2)]
            Rstg = [Res(f"stg{i}") for i in range(4)]
            Rrow = [Res("row0"), Res("row1")]
            pb = 7
            reserved.add(pb)
            for ll in (range(1) if MODS_INTERLEAVE else range(depth)):
                loads, blocks, trans, final = mod_tasks(ll, stg, Rstg, rowb, Rrow, pb)
                for b in range(3):
                    loads[b]()
                for b in range(12):
                    blocks[b]()
                    if b + 3 < 12:
                        loads[b + 3]()
                    if b >= 1:
                        trans[b - 1]()
                trans[11]()
                final()
            reserved.discard(pb)

        emit_mods()
        P.barrier()

        def emit_rstd(pb, W, n, rtmp, R_rtmp):
            P.op("act", ("activation", dict(out=rtmp[:, :W], in_=psb[pb][:, :W], func=AF.Sqrt,
                                               bias=epsc[:, 0:1], scale=1.0 / n)),
                 reads=[Rps[pb], R_const], writes=[R_rtmp])
            P.op("dve", ("reciprocal", dict(out=psb[pb][:, :W], in_=rtmp[:, :W])),
                 reads=[R_rtmp], writes=[Rps[pb]])


        def emit_norm_a(xt, R_x, W, sq, R_sq):
            P.op("act", ("activation", dict(out=sq[:, :, :W], in_=xt[:, :, :W], func=AF.Square)),
                 reads=[R_x], writes=[R_sq])

        def emit_norm_b(xt, R_x, W, Acol, Bcol, hout, R_h, sq, R_sq, rtmp, R_rtmp):
            pb = nb()
            fns = [("matmul", dict(out=psb[pb][:, :W], lhsT=ones_b, rhs=sq[:, k, :W],
                                   start=(k == 0), stop=(k == 7))) for k in range(8)]
            P.op("pe", fns, reads=[R_sq, R_const], writes=[Rps[pb]])
            emit_rstd(pb, W, float(D), rtmp, R_rtmp)
            rb = psb[pb][:, :W].unsqueeze(1).to_broadcast([128, 8, W])
            P.op("dve", ("tensor_tensor", dict(out=xt[:, :, :W], in0=xt[:, :, :W], in1=rb, op=ALU.mult)),
                 reads=[Rps[pb], R_x], writes=[R_x])
            for k in range(8):
                P.op("act", ("activation", dict(out=hout[:, k, :W], in_=xt[:, k, :W], func=AF.Identity,
                                                bias=Bcol(k), scale=Acol(k))),
                     reads=[R_x, Rm(), R_const], writes=[R_h])

        def emit_norm_mod(xt, R_x, W, Acol, Bcol, gsel, hout, R_h, sq, R_sq, rtmp, R_rtmp):
            emit_norm_a(xt, R_x, W, sq, R_sq)
            emit_norm_b(xt, R_x, W, Acol, Bcol, hout, R_h, sq, R_sq, rtmp, R_rtmp)

        sample_seqs = [dict(t0=0, L=LS, rope=True, ctx=True, g=0, oi=None)]
        prompt_seqs = [dict(t0=LS, L=LP, rope=False, ctx=False, g=1, oi=0),
                       dict(t0=LS + LP, L=LP, rope=False, ctx=False, g=1, oi=1)]

        out_tokens = []

        def mixer_pass(l, seqs, xsrc, xdst, R_xsrc, R_xdst, hook=None, pre=None):
            AR.off = reg0
            nkeys = sum(s["L"] + (PAST if s["ctx"] else 0) for s in seqs)
            ntok = sum(s["L"] for s in seqs)
            nseq = len(seqs)
            WOUT = AR.alloc([8, 1024], BF16)
            KN = AR.alloc([4, nkeys], BF16)
            VV = AR.alloc([nkeys // 128, 512], BF16)
            KR = AR.alloc([nkeys], BF16)
            QN = AR.alloc([3, ntok], BF16)
            ZP = AR.alloc([2, ntok + 16 * nseq], F32)
            AG = AR.alloc([2, ntok + 30 * nseq], BF16)
            XT = AR.alloc([8, 512], F32)
            SCR = AR.alloc([4, 512], F32)
            SQ = SCR.rearrange("p a b -> p (a b)").bitcast(BF16).rearrange("p (a b) -> p a b", b=512)
            YPC = AR.alloc([4, 512], BF16)
            RT = AR.alloc([512], F32)
            slot0 = AR.off
            WIN = AR.alloc([8, INW], BF16)
            WUKV = AR.alloc([2, 1024], BF16)
            HH = AR.alloc([8, 512], BF16)
            endA = AR.off
            AR.off = slot0
            DIAG = AR.alloc([62, 128], BF16)
            QNP = AR.alloc([4, 512], BF16)
            QRP = AR.alloc([4, 512], BF16)
            YA = AR.alloc([4, 512], BF16)
            PT = [AR.alloc([512], BF16) for _ in range(4)]
            WUQ = AR.alloc([3, 1024], BF16)
            if hook is not None:
                assert max(endA, AR.off) <= FFN_HI, (endA, AR.off, FFN_HI)
            R = {n: Res(n) for n in ("KN", "VV", "KR", "QN", "ZP", "AG", "WOUT", "XT", "SCR", "YPC", "RT",
                                     "WIN", "WUKV", "HH", "DIAG", "QNP", "QRP", "YA", "WUQ", "CK", "DD")}
            RPT = [Res(f"PT{i}") for i in range(4)]
            RYA = [Res(f"YA{i}") for i in range(4)]
            RQN = [Res(f"QNP{i}") for i in range(4)]
            RQR = [Res(f"QRP{i}") for i in range(4)]
            R["D32"] = Res("D32")
            R["U32"] = Res("U32")
            g = seqs[0]["g"]

            kc = 0
            zc = 0
            ac = 0
            qc = 0
            for s in seqs:
                s["kc"] = kc
                kc += s["L"] + (PAST if s["ctx"] else 0)
                s["zc"] = zc
                zc += s["L"] + 16
                s["ac"] = ac
                ac += s["L"] + 30
                s["qc"] = qc
                qc += s["L"]

            if pre is not None:
                assert pre["off"] == slot0, (pre["off"], slot0)
                R["WIN"], R["WUKV"] = pre["R_WIN"], pre["R_WUKV"]
            else:
                P.op("pool", ("dma_start", dict(out=WIN, in_=win_d[l], max_dma_last_dim=4096)), writes=[R["WIN"]], dma=True)
                P.op("pool", ("dma_start", dict(out=WUKV, in_=wukv_d[l], max_dma_last_dim=4096)), writes=[R["WUKV"]], dma=True)
            if hook is None:
                P.op("pool", ("dma_start", dict(out=WOUT, in_=wout_d[l], max_dma_last_dim=4096)), writes=[R["WOUT"]], dma=True)
                P.op("pool", ("dma_start", dict(out=pwbd, in_=pwbd_d[l])), writes=[R_pwbd], dma=True)
            P.op("pool", ("memset", dict(ap=ZP, constant=0.0)), writes=[R["ZP"]])
            P.op("pool", ("memset", dict(ap=AG, constant=0.0)), writes=[R["AG"]])
            if hook is not None:
                hook()

            CK = SCR

            def kv_project(ckb, W, keycol, R_cks):
                for h in range(4):
                    pb = nb()
                    fns = [("matmul", dict(out=psb[pb][:, :W], lhsT=WUKV[:, k, h * 128:(h + 1) * 128],
                                           rhs=ckb[:, k, :W], start=(k == 0), stop=(k == 1))) for k in range(2)]
                    P.op("pe", fns, reads=[R["WUKV"]] + R_cks, writes=[Rps[pb]])
                    P.op("act", ("activation", dict(out=KN[:, h, keycol:keycol + W], in_=psb[pb][:, :W], func=AF.Copy)),
                         reads=[Rps[pb]], writes=[R["KN"]])
                for i in range(W // 128):
                    pb = nb()
                    fns = [("matmul", dict(out=psb[pb][:, :], lhsT=ckb[:, k, i * 128:(i + 1) * 128],
                                           rhs=WUKV[:, k, 512:1024], start=(k == 0), stop=(k == 1))) for k in range(2)]
                    P.op("pe", fns, reads=[R["WUKV"]] + R_cks, writes=[Rps[pb]])
                    kt = keycol // 128 + i
                    P.op("dve", ("tensor_copy", dict(out=VV[:, kt, :], in_=psb[pb][:, :])),
                         reads=[Rps[pb]], writes=[R["VV"]])

            RXT = [Res("XT0"), Res("XT1")]
            RHH = [Res("HH0"), Res("HH1")]
            RSC = [Res("SC0"), Res("SC1")]
            RRT = [Res("RT0"), Res("RT1")]
            RYP = [Res("YP0"), Res("YP1")]
            SCRf = SCR.rearrange("p a b -> p (a b)")
            P.op("pool", ("memset", dict(ap=KR[64:128, :], constant=0.0)), writes=[R["KR"]])

            for s in seqs:
                if s["ctx"]:
                    ckc = HH[:, 0:2, :]
                    kcol = s["kc"] + s["L"]
                    P.op("pool", ("dma_start", dict(out=ckc, in_=cckv_d[l], max_dma_last_dim=4096)), writes=RHH, dma=True)
                    P.op("pool", ("dma_start", dict(out=KR[0:64, kcol:kcol + PAST], in_=ckr_d[l])),
                         writes=[R["KR"]], dma=True)
                    kv_project(ckc, PAST, kcol, RHH)

            def tileA(s, a, par):
                W = 256
                L = s["L"]
                ta = s["t0"] + a
                XTp = XT[:, :, par * 256:(par + 1) * 256]
                HHp = HH[:, :, par * 256:(par + 1) * 256]
                SCp = SCRf[:, par * 1024:(par + 1) * 1024]
                SQp = SCp.bitcast(BF16).rearrange("p (a b) -> p a b", b=256)
                CKF = SCp[:, 512:1024].rearrange("p (a b) -> p a b", b=256)
                T1 = SCp[0:64, 512:768]
                T2 = SCp[0:64, 768:1024]
                RTp = RT[:, par * 256:(par + 1) * 256]
                CKB = YPC[:, 2 * par:2 * par + 2, 0:256]
                R_x, R_h, R_s, R_r, R_y = RXT[par], RHH[par], RSC[par], RRT[par], RYP[par]

                def norm_fn():
                    P.op("sp", ("dma_start", dict(out=XTp, in_=xview(xsrc)[:, :, ta:ta + W])),
                         reads=[R_xsrc], writes=[R_x], dma=True)
                    emit_norm_a(XTp, R_x, W, SQp, R_s)
                    yield
                    pb = nb()
                    reserved.add(pb)
                    fns = [("matmul", dict(out=psb[pb][:, :W], lhsT=ones_b, rhs=SQp[:, k, :W],
                                           start=(k == 0), stop=(k == 7))) for k in range(8)]
                    P.op("pe", fns, reads=[R_s, R_const], writes=[Rps[pb]])
                    yield
                    P.op("act", ("activation", dict(out=RTp, in_=psb[pb][:, :W], func=AF.Sqrt,
                                                    bias=epsc[:, 0:1], scale=1.0 / D)),
                         reads=[Rps[pb], R_const], writes=[R_r])
                    yield
                    P.op("dve", ("reciprocal", dict(out=psb[pb][:, :W], in_=RTp)), reads=[R_r], writes=[Rps[pb]])
                    yield
                    rb = psb[pb][:, :W].unsqueeze(1).to_broadcast([128, 8, W])
                    P.op("dve", ("tensor_tensor", dict(out=XTp, in0=XTp, in1=rb, op=ALU.mult)),
                         reads=[Rps[pb], R_x], writes=[R_x])
                    reserved.discard(pb)
                    yield
                    for k in range(8):
                        P.op("act", ("activation", dict(out=HHp[:, k, :], in_=XTp[:, k, :], func=AF.Identity,
                                                        bias=MOD[:, l, 0 + k, g:g + 1], scale=AMOD[:, l * 2 + 0, k, g:g + 1])),
                             reads=[R_x, Rm(), R_const], writes=[R_h])
                        if k % 2 == 1:
                            yield

                def z_fn():
                    def zmm(c0, M, pb):
                        fns = [("matmul", dict(out=pb[0][0:M, :], lhsT=WIN[:, k, c0:c0 + M], rhs=HHp[:, k, :],
                                               start=(k == 0), stop=(k == 7))) for k in range(8)]
                        P.op("pe", fns, reads=[R["WIN"], R_h], writes=[pb[1]])

                    for c in range(2):
                        pb = nbh()
                        zmm(c * 128, 128, pb)
                        zo = s["zc"] + 8 + a
                        P.op("act", ("activation", dict(out=ZP[:, c, zo:zo + W], in_=pb[0], func=AF.Copy)),
                             reads=[pb[1]], writes=[R["ZP"]])
                        yield

                    def latent_norm(c0, nch, gname, outs):
                        pbs = []
                        for c in range(nch):
                            pb = nbh()
                            reserved.add(pb[2])
                            pbs.append(pb)
                            zmm(c0 + c * 128, 128, pb)
                            P.op("act", ("activation", dict(out=SQp[:, c, :], in_=pb[0], func=AF.Square)),
                                 reads=[pb[1]], writes=[R_s])
                            yield
                        ps_s = nbh()
                        fns = [("matmul", dict(out=ps_s[0], lhsT=ones_b, rhs=SQp[:, c, :],
                                               start=(c == 0), stop=(c == nch - 1))) for c in range(nch)]
                        P.op("pe", fns, reads=[R_s, R_const], writes=[ps_s[1]])
                        P.op("act", ("activation", dict(out=RTp, in_=ps_s[0], func=AF.Sqrt,
                                                        bias=epsc[:, 0:1], scale=1.0 / (nch * 128))),
                             reads=[ps_s[1], R_const], writes=[R_r])
                        P.op("dve", ("reciprocal", dict(out=RTp, in_=RTp)), reads=[R_r], writes=[R_r])
                        yield
                        for c in range(nch):
                            for (oap, Ro) in outs(c):
                                P.op("dve", ("scalar_tensor_tensor", dict(
                                    out=oap, in0=pbs[c][0], scalar=vcol(l, gname, c), in1=RTp,
                                    op0=ALU.mult, op1=ALU.mult)),
                                    reads=[pbs[c][1], R_r, R_const], writes=[Ro])
                            reserved.discard(pbs[c][2])
                        yield

                    qo = s["qc"] + a
                    yield from latent_norm(256, 3, "qg", lambda c: [(QN[:, c, qo:qo + W], R["QN"])])
                    if s["oi"] is not None:
                        yield from latent_norm(640, 2, "kvg", lambda c: [(CKB[:, c, :], R_y), (CKF[:, c, :], R_s)])
                        oi = s["oi"]
                        o = P.op("sp", ("dma_start", dict(
                            out=sckv_d[l, oi].rearrange("k p t -> p k t")[:, :, a:a + W], in_=CKF)),
                            reads=[R_s], dma=True)
                        out_tokens.append(o.tok)
                    else:
                        yield from latent_norm(640, 2, "kvg", lambda c: [(CKB[:, c, :], R_y)])

                    kcol = s["kc"] + a
                    pb = nbh()
                    zmm(896, 64, pb)
                    if s["rope"]:
                        pb2 = nbh()
                        zmm(1472, 64, pb2)
                        P.op("dve", ("tensor_tensor", dict(out=T1, in0=pb[0][0:64, :], in1=cs[0:64, a:a + W], op=ALU.mult)),
                             reads=[pb[1], R_const], writes=[R_s])
                        P.op("dve", ("tensor_tensor", dict(out=T2, in0=pb2[0][0:64, :], in1=cs[64:128, a:a + W], op=ALU.mult)),
                             reads=[pb2[1], R_const], writes=[R_s])
                        P.op("dve", ("tensor_tensor", dict(out=KR[0:64, kcol:kcol + W], in0=T1, in1=T2, op=ALU.add)),
                             reads=[R_s], writes=[R["KR"]])
                    else:
                        P.op("act", ("activation", dict(out=KR[0:64, kcol:kcol + W], in_=pb[0][0:64, :], func=AF.Copy)),
                             reads=[pb[1]], writes=[R["KR"]])
                        P.op("act", ("activation", dict(out=T1, in_=pb[0][0:64, :], func=AF.Copy)),
                             reads=[pb[1]], writes=[R_s])
                        oi = s["oi"]
                        o = P.op("sp", ("dma_start", dict(out=skr_d[l, oi][:, a:a + W], in_=T1)), reads=[R_s], dma=True)
                        out_tokens.append(o.tok)
                    yield

                    for c in range(2):
                        pbg = nbh()
                        zmm(1216 + c * 128, 128, pbg)
                        P.op("act", ("activation", dict(out=RTp, in_=pbg[0], func=AF.Sigmoid)),
                             reads=[pbg[1]], writes=[R_r])
                        pba = nbh()
                        zmm(960 + c * 128, 128, pba)
                        ao = s["ac"] + 15 + a
                        P.op("dve", ("tensor_tensor", dict(out=AG[:, c, ao:ao + W], in0=pba[0], in1=RTp, op=ALU.mult)),
                             reads=[pba[1], R_r], writes=[R["AG"]])
                        yield
                    yield from kv_project_g(CKB, W, s["kc"] + a, [R_y])

                return norm_fn, z_fn

            def kv_project_g(ckb, W, keycol, R_cks):
                for h in range(4):
                    pb = nb()
                    fns = [("matmul", dict(out=psb[pb][:, :W], lhsT=WUKV[:, k, h * 128:(h + 1) * 128],
                                           rhs=ckb[:, k, :W], start=(k == 0), stop=(k == 1))) for k in range(2)]
                    P.op("pe", fns, reads=[R["WUKV"]] + R_cks, writes=[Rps[pb]])
                    P.op("act", ("activation", dict(out=KN[:, h, keycol:keycol + W], in_=psb[pb][:, :W], func=AF.Copy)),
                         reads=[Rps[pb]], writes=[R["KN"]])
                    yield
                for i in range(W // 128):
                    pb = nb()
                    fns = [("matmul", dict(out=psb[pb][:, :], lhsT=ckb[:, k, i * 128:(i + 1) * 128],
                                           rhs=WUKV[:, k, 512:1024], start=(k == 0), stop=(k == 1))) for k in range(2)]
                    P.op("pe", fns, reads=[R["WUKV"]] + R_cks, writes=[Rps[pb]])
                    kt = keycol // 128 + i
                    P.op("dve", ("tensor_copy", dict(out=VV[:, kt, :], in_=psb[pb][:, :])),
                         reads=[Rps[pb]], writes=[R["VV"]])
                    yield

            tilesA = []
            for s in seqs:
                for a in range(0, s["L"], 256):
                    tilesA.append(tileA(s, a, len(tilesA) % 2))
            for _ in tilesA[0][0]():
                pass
            for ti in range(len(tilesA)):
                ng = tilesA[ti + 1][0]() if ti + 1 < len(tilesA) else None
                step = 0
                for _ in tilesA[ti][1]():
                    step += 1
                    if ng is not None and step >= 2:
                        next(ng, None)
                if ng is not None:
                    for _ in ng:
                        pass

            P.barrier()
            P.op("pool", ("dma_start", dict(out=WUQ, in_=wuq_d[l], max_dma_last_dim=4096)), writes=[R["WUQ"]], dma=True)
            P.op("pool", ("dma_start", dict(out=DIAG, in_=cdiag_d[l], max_dma_last_dim=4096)), writes=[R["DIAG"]], dma=True)
            P.op("pool", ("memset", dict(ap=QRP[64:128, :, :], constant=0.0)), writes=RQR)
            D32 = SCR[:, 0:2, :]
            U32 = SCR[:, 2:4, :]
            XTf = XT.rearrange("p a b -> p (a b)")
            TA = XTf[:, 0:1056].rearrange("p (a b) -> p a b", b=528)
            TB = XTf[:, 1056:2112].rearrange("p (a b) -> p a b", b=528)
            TC = XTf[:, 2112:2640]
            TD = XTf[:, 2640:3152]
            DB = YA[:, 2:4, :]
            RD, RU = R["D32"], R["U32"]

            def qproj(s, a):
                W = min(512, s["L"] - a)
                qo = s["qc"] + a
                rot[0] = 0
                saved = set(reserved)
                reserved.update({3, 4, 5, 6})
                RD, RU = R["D32"], R["U32"]
                for h in range(4):
                    pb = nb()
                    fns = [("matmul", dict(out=psb[pb][:, :W], lhsT=WUQ[:, k, h * 128:(h + 1) * 128],
                                           rhs=QN[:, k, qo:qo + W], start=(k == 0), stop=(k == 2))) for k in range(3)]
                    P.op("pe", fns, reads=[R["WUQ"], R["QN"]], writes=[Rps[pb]])
                    P.op("dve", ("tensor_copy", dict(out=QNP[:, h, :W], in_=psb[pb][:, :W])),
                         reads=[Rps[pb]], writes=[RQN[h]])
                    pr = nb()
                    fns = [("matmul", dict(out=psb[pr][0:64, :W], lhsT=WUQ[:, k, 512 + h * 64:512 + (h + 1) * 64],
                                           rhs=QN[:, k, qo:qo + W], start=(k == 0), stop=(k == 2))) for k in range(3)]
                    P.op("pe", fns, reads=[R["WUQ"], R["QN"]], writes=[Rps[pr]])
                    if s["rope"]:
                        pr2 = nb()
                        fns = [("matmul", dict(out=psb[pr2][0:64, :W], lhsT=WUQ[:, k, 768 + h * 64:768 + (h + 1) * 64],
                                               rhs=QN[:, k, qo:qo + W], start=(k == 0), stop=(k == 2))) for k in range(3)]
                        P.op("pe", fns, reads=[R["WUQ"], R["QN"]], writes=[Rps[pr2]])
                        T1 = U32[0:64, 0, :]
                        T2q = U32[0:64, 1, :]
                        P.op("dve", ("tensor_tensor", dict(out=T1[:, :W], in0=psb[pr][0:64, :W], in1=cs[0:64, a:a + W], op=ALU.mult)),
                             reads=[Rps[pr], R_const], writes=[RU])
                        P.op("dve", ("tensor_tensor", dict(out=T2q[:, :W], in0=psb[pr2][0:64, :W], in1=cs[64:128, a:a + W], op=ALU.mult)),
                             reads=[Rps[pr2], R_const], writes=[RU])
                        P.op("dve", ("tensor_tensor", dict(out=QRP[0:64, h, :W], in0=T1[:, :W], in1=T2q[:, :W], op=ALU.add)),
                             reads=[RU], writes=[RQR[h]])
                    else:
                        P.op("dve", ("tensor_copy", dict(out=QRP[0:64, h, :W], in_=psb[pr][0:64, :W])),
                             reads=[Rps[pr]], writes=[RQR[h]])


                reserved.clear()
                reserved.update(saved)

            tilesE = [(s, a) for s in seqs for a in range(0, s["L"], 512)]
            qproj(*tilesE[0])
            for ti, (s, a) in enumerate(tilesE):
                if True:
                    L = s["L"]
                    nkt = (L + (PAST if s["ctx"] else 0)) // 128
                    kbase = s["kc"]
                    W = min(512, L - a)
                    ta = s["t0"] + a
                    zb = s["zc"] + 8 + a
                    ab = s["ac"] + a
                    qo = s["qc"] + a

                    P.op("pool", ("tensor_tensor", dict(out=TA[:, :, 0:W + 14], in0=ZP[:, :, zb - 8:zb + W + 6],
                                                        in1=ZP[:, :, zb - 7:zb + W + 7], op=ALU.add)),
                         reads=[R["ZP"]], writes=[R["XT"]])
                    P.op("pool", ("tensor_tensor", dict(out=TB[:, :, 0:W + 12], in0=TA[:, :, 0:W + 12],
                                                        in1=TA[:, :, 2:W + 14], op=ALU.add)),
                         reads=[R["XT"]], writes=[R["XT"]])
                    P.op("pool", ("tensor_tensor", dict(out=TC[:, 0:W + 8], in0=TB[:, 1, 0:W + 8],
                                                        in1=TB[:, 1, 4:W + 12], op=ALU.add)),
                         reads=[R["XT"]], writes=[R["XT"]])
                    P.op("pool", ("tensor_tensor", dict(out=TD[:, 0:W], in0=TC[:, 0:W], in1=TC[:, 8:W + 8], op=ALU.add)),
                         reads=[R["XT"]], writes=[R["XT"]])
                    srcs = [(TA[0:64, 0, 7:7 + W], 0, 0, 64), (TB[64:128, 0, 6:6 + W], 0, 64, 128),
                            (TC[0:64, 4:4 + W], 1, 0, 64), (TD[64:128, 0:W], 1, 64, 128)]
                    for (sap, c, p0, p1) in srcs:
                        P.op("dve", ("scalar_tensor_tensor", dict(
                            out=D32[p0:p1, c, :W], in0=sap, scalar=invw[p0:p1, c:c + 1], in1=ZP[p0:p1, c, zb:zb + W],
                            op0=ALU.mult, op1=ALU.subtract)),
                            reads=[R["XT"], R["ZP"], R_const], writes=[RD])
                    for side, cond, c0 in ((0, a == 0, 0), (1, a + W == L, W - 8)):
                        if not cond:
                            continue
                        dsl = D32[:, :, c0:c0 + 8]
                        zsl = ZP[:, :, zb + c0:zb + c0 + 8]
                        P.op("dve", ("tensor_tensor", dict(out=dsl, in0=dsl, in1=zsl, op=ALU.add)),
                             reads=[R["ZP"]], writes=[RD])
                        P.op("dve", ("tensor_tensor", dict(out=dsl, in0=dsl, in1=poolf[:, side, :, :], op=ALU.mult)),
                             reads=[R_const], writes=[RD])
                        P.op("dve", ("tensor_tensor", dict(out=dsl, in0=dsl, in1=zsl, op=ALU.subtract)),
                             reads=[R["ZP"]], writes=[RD])
                    P.op("dve", ("tensor_copy", dict(out=DB[:, :, :W], in_=D32[:, :, :W])),
                         reads=[RD], writes=[RYA[2], RYA[3]])
                    P.op("sp", ("dma_start", dict(out=XT[:, :, :W], in_=xview(xsrc)[:, :, ta:ta + W])),
                         reads=[R_xsrc], writes=[R["XT"]], dma=True)

                    def side_conv():
                        for c in range(2):
                            pb = 7
                            fns = [("matmul", dict(out=psb[pb][:, :W], lhsT=DIAG[:, c * 31 + k, :], rhs=AG[:, c, ab + k:ab + k + W],
                                                   start=(k == 0), stop=(k == 30))) for k in range(31)]
                            P.op("pe", fns, reads=[R["DIAG"], R["AG"]], writes=[Rps[pb]])
                            P.op("dve", ("tensor_scalar", dict(out=U32[:, c, :W], in0=psb[pb][:, :W], scalar1=vcol(l, "convb", c),
                                                               scalar2=None, op0=ALU.add)),
                                 reads=[Rps[pb], R_const], writes=[RU])

                    def side_pool_ln1():
                        for c in range(2):
                            pb = 7
                            P.op("pe", ("matmul", dict(out=psb[pb][:, :W], lhsT=pwbd[:, c, :], rhs=DB[:, c, :W], start=True, stop=True)),
                                 reads=[RYA[2], RYA[3], R_pwbd], writes=[Rps[pb]])
                            P.op("dve", ("tensor_scalar", dict(out=YPC[:, c, :W], in0=psb[pb][:, :W], scalar1=vcol(l, "pscale", c),
                                                               scalar2=None, op0=ALU.mult)),
                                 reads=[Rps[pb], R_const], writes=[R["YPC"]])
                        pm = 7
                        fns = [("matmul", dict(out=psb[pm][:, :W], lhsT=ones_f, rhs=U32[:, c, :W], start=(c == 0), stop=(c == 1)))
                               for c in range(2)]
                        P.op("pe", fns, reads=[RU, R_const], writes=[Rps[pm]])
                        for c in range(2):
                            P.op("dve", ("scalar_tensor_tensor", dict(out=U32[:, c, :W], in0=psb[pm][:, :W], scalar=-1.0 / 256.0,
                                                                      in1=U32[:, c, :W], op0=ALU.mult, op1=ALU.add)),
                                 reads=[Rps[pm], RU], writes=[RU])
                        P.op("dve", ("tensor_tensor", dict(out=D32[:, :, :W], in0=U32[:, :, :W], in1=U32[:, :, :W], op=ALU.mult)),
                             reads=[RU], writes=[RD])

                    def side_ln2():
                        pv = 7
                        fns = [("matmul", dict(out=psb[pv][:, :W], lhsT=ones_f, rhs=D32[:, c, :W], start=(c == 0), stop=(c == 1)))
                               for c in range(2)]
                        P.op("pe", fns, reads=[RD, R_const], writes=[Rps[pv]])
                        LT = D32[:, 0, :]
                        P.op("act", ("activation", dict(out=LT[:, :W], in_=psb[pv][:, :W], func=AF.Sqrt,
                                                        bias=epsc[:, 0:1], scale=1.0 / 256.0)),
                             reads=[Rps[pv], R_const], writes=[RD])
                        P.op("dve", ("reciprocal", dict(out=LT[:, :W], in_=LT[:, :W])), reads=[RD], writes=[RD])
                        for c in range(2):
                            P.op("dve", ("tensor_tensor", dict(out=U32[:, c, :W], in0=U32[:, c, :W], in1=LT[:, :W], op=ALU.mult)),
                                 reads=[RD, RU], writes=[RU])

                    def side_silu():
                        for c in range(2):
                            P.op("act", ("activation", dict(out=YPC[:, 2 + c, :W], in_=U32[:, c, :W], func=AF.Silu,
                                                            bias=vcol(l, "lnb", c), scale=vcol(l, "lng", c))),
                                 reads=[RU, R_const], writes=[R["YPC"]])

                    sides = {0: [side_conv], 1: [side_pool_ln1], 2: [side_ln2], 3: [side_silu]}

                    for h in range(4):
                        po = 3 + (h % 2)
                        psum_ = 5 + (h % 2)

                        def s_step(kt, h=h):
                            pss = kt % 3
                            kc0 = kbase + kt * 128
                            P.op("pe", [("matmul", dict(out=psb[pss][:, :W], lhsT=KN[:, h, kc0:kc0 + 128],
                                                        rhs=QNP[:, h, :W], start=True, stop=False)),
                                        ("matmul", dict(out=psb[pss][:, :W], lhsT=KR[:, kc0:kc0 + 128],
                                                        rhs=QRP[:, h, :W], start=False, stop=True))],
                                 reads=[R["KN"], R["KR"], RQN[h], RQR[h]], writes=[Rps[pss]])
                            pt = PT[kt % 4]
                            P.op("act", ("activation", dict(out=pt[:, :W], in_=psb[pss][:, :W], func=AF.Exp, scale=ATT_SCALE)),
                                 reads=[Rps[pss]], writes=[RPT[kt % 4]])

                        def pv_step(kt, h=h, po=po, psum_=psum_):
                            ktg = (kbase + kt * 128) // 128
                            pt = PT[kt % 4]
                            P.op("pe", [("matmul", dict(out=psb[po][:, :W], lhsT=VV[:, ktg, h * 128:(h + 1) * 128], rhs=pt[:, :W],
                                                        start=(kt == 0), stop=(kt == nkt - 1))),
                                        ("matmul", dict(out=psb[psum_][:, :W], lhsT=ones_b, rhs=pt[:, :W],
                                                        start=(kt == 0), stop=(kt == nkt - 1)))],
                                 reads=[R["VV"], RPT[kt % 4], R_const], writes=[Rps[po], Rps[psum_]])

                        LA = 2
                        for kt in range(nkt + LA):
                            if kt < nkt:
                                s_step(kt)
                            if kt >= LA:
                                pv_step(kt - LA)
                        P.op("dve", ("reciprocal", dict(out=RT[:, :W], in_=psb[psum_][:, :W])),
                             reads=[Rps[psum_]], writes=[R["RT"]])
                        P.op("dve", ("tensor_tensor", dict(out=YA[:, h, :W], in0=psb[po][:, :W], in1=RT[:, :W], op=ALU.mult)),
                             reads=[Rps[po], R["RT"]], writes=[RYA[h]])
                        for f in sides[h]:
                            f()
                    rot[0] = 0
                    if ti + 1 < len(tilesE):
                        qproj(*tilesE[ti + 1])
                    ycat = [YPC[:, 0, :], YPC[:, 1, :], YA[:, 0, :], YA[:, 1, :], YA[:, 2, :], YA[:, 3, :], YPC[:, 2, :], YPC[:, 3, :]]
                    for m in range(8):
                        pb = 7 if m % 2 == 0 else 0
                        kord = [0, 1, 6, 7, 2, 3, 4, 5]
                        fns = [("matmul", dict(out=psb[pb][:, :W], lhsT=WOUT[:, kc_, m * 128:(m + 1) * 128],
                                               rhs=ycat[kc_][:, :W], start=(ii == 0), stop=(ii == 7))) for ii, kc_ in enumerate(kord)]
                        P.op("pe", fns, reads=[R["WOUT"], R["YPC"]] + RYA, writes=[Rps[pb]])
                        P.op("dve", ("scalar_tensor_tensor", dict(
                            out=XT[:, m, :W], in0=psb[pb][:, :W], scalar=MOD[:, l, 16 + m, g:g + 1], in1=XT[:, m, :W],
                            op0=ALU.mult, op1=ALU.add)),
                            reads=[Rps[pb], R["XT"], Rm()], writes=[R["XT"]])
                    P.op("sp", ("dma_start", dict(out=xview(xdst)[:, :, ta:ta + W], in_=XT[:, :, :W])),
                         reads=[R["XT"]], writes=[R_xdst], dma=True)
            P.barrier()

        def ffn_phase(l, xsrc, xdst, R_xsrc, R_xdst, nxt=None):
            WMAX = 412
            AR.off = reg0
            GV = AR.alloc([NJ, 930], BF16)
            WDN = [AR.alloc([NJ, 128], BF16) for _ in range(3)]
            ACC = [AR.alloc([2, WMAX], F32) for _ in range(4)]
            SG = [AR.alloc([WMAX], F32) for _ in range(4)]
            XM = [AR.alloc([WMAX], F32) for _ in range(6)]
            if MODS_INTERLEAVE and l + 1 < depth:
                mstg = [AR.alloc([8, 512], BF16) for _ in range(2)]
            assert AR.off <= FFN_HI, AR.off
            AR.off = FFN_HI
            WUP = [AR.alloc([8, 256], BF16) for _ in range(4)]
            XH = AR.alloc([8, WMAX], F32)
            SQH = AR.alloc([8, WMAX], BF16)
            H2 = AR.alloc([3 * 8, WMAX], BF16)
            RTH = AR.alloc([WMAX], F32)
            mt = None
            if MODS_INTERLEAVE and l + 1 < depth:
                mrow = [AR.alloc([512], F32) for _ in range(2)]
                reserved.add(7)
                mt = mod_tasks(l + 1, mstg, [Res("ms0"), Res("ms1")], mrow, [Res("mr0"), Res("mr1")], 7)
            R = {n: Res(n) for n in ("GV", "XH", "SQH", "RTH")}
            RH2 = [Res(f"H2{i}") for i in range(3)]
            RGV = [Res(f"GV{j}") for j in range(NJ)]
            RWUP = [Res(f"WUP{i}") for i in range(4)]
            RWDN = [Res(f"WDN{i}") for i in range(3)]
            RACC = [[Res(f"ACC{i}g"), Res(f"ACC{i}v")] for i in range(4)]
            RSG = [Res(f"SG{i}") for i in range(4)]
            RXM = [Res(f"XM{i}") for i in range(6)]

            def subt(t0, L, a, b, g):
                return dict(t0=t0, L=L, a=a, b=b, g=g, hl=1 if a > 0 else 0, hr=1 if b < L else 0)

            sb = [0, 410, 820, 1230, 1640, 2048]
            ssub = [subt(0, LS, sb[i], sb[i + 1], 0) for i in range(5)]
            psub = [subt(LS, LP, 0, LP, 1), subt(LS + LP, LP, 0, LP, 1)]
            supers = [ssub[0:2], ssub[2:4], [ssub[4]] + psub]
            cnt_acc = [0]
            pend = []
            cnt_xm = [0]
            def load_wup(j):
                P.op("pool", ("dma_start", dict(out=WUP[j % 4], in_=wup_d[l, j], max_dma_last_dim=4096)),
                     writes=[RWUP[j % 4]], dma=True)

            def load_wdn(m):
                P.op("pool", ("dma_start", dict(out=WDN[m % 3], in_=wdn_d[l, m], max_dma_last_dim=4096)),
                     writes=[RWDN[m % 3]], dma=True)

            def h2_parts(ST):
                parts = []
                gc = 0
                for i, t in enumerate(ST):
                    t["gc"] = gc
                    gc += t["b"] - t["a"]
                    Wh = t["b"] - t["a"] + t["hl"] + t["hr"]
                    ca = t["t0"] + t["a"] - t["hl"]
                    g = t["g"]
                    h2v = H2[:, i * 8:(i + 1) * 8, :]

                    def pa(Wh=Wh, ca=ca):
                        P.op("sp", ("dma_start", dict(out=XH[:, :, :Wh], in_=xview(xsrc)[:, :, ca:ca + Wh])),
                             reads=[R_xsrc], writes=[R["XH"]], dma=True)
                        emit_norm_a(XH, R["XH"], Wh, SQH, R["SQH"])

                    def pb_(Wh=Wh, g=g, h2v=h2v, i=i):
                        emit_norm_b(XH, R["XH"], Wh,
                                    lambda k: AMOD[:, l * 2 + 1, k, g:g + 1], lambda k: MOD[:, l, 24 + k, g:g + 1],
                                    h2v, RH2[i], SQH, R["SQH"], RTH, R["RTH"])
                    parts.append((pa, pb_))
                return parts

            def emit_h2(ST):
                for pa, pb_ in h2_parts(ST):
                    pa()
                    pb_()

            for jj in range(3):
                load_wup(jj)
            emit_h2(supers[0])
            yield
            for sti, ST in enumerate(supers):
                if sti == 0 and mt is not None:
                    mt[0][0]()
                    mt[0][1]()
                for j in range(NJ):
                    sl = j % 4
                    if j + 3 < NJ:
                        load_wup(j + 3)
                    if sti == 0 and mt is not None:
                        mb = (j - 1) // 2 if j % 2 == 1 else None
                        if j == NJ - 2:
                            mb = 10
                        if j == NJ - 1:
                            mb = 11
                        if mb is not None and mb < 12:
                            mt[1][mb]()
                            if mb + 2 < 12:
                                mt[0][mb + 2]()
                            if mb >= 1:
                                mt[2][mb - 1]()
                    if j == NJ - 3:
                        load_wdn(0)
                    if j == NJ - 2:
                        load_wdn(1)
                    for i, t in enumerate(ST):
                        W = t["b"] - t["a"]
                        hl, hr = t["hl"], t["hr"]
                        Wh = W + hl + hr
                        h2v = H2[:, i * 8:(i + 1) * 8, :]
                        ai = cnt_acc[0] % 4
                        cnt_acc[0] += 1
                        acc = ACC[ai]
                        pbs = []
                        for half in range(2):
                            pb = nb()
                            pbs.append(pb)
                            fns = [("matmul", dict(out=psb[pb][:, :Wh], lhsT=WUP[sl][:, k, half * 128:(half + 1) * 128], rhs=h2v[:, k, :Wh],
                                start=(k == 0), stop=(k == 7))) for k in range(8)]
                            P.op("pe", fns, reads=[RWUP[sl], RH2[i]], writes=[Rps[pb]])
                            ch = half * NJ + j
                            P.op("act", ("activation", dict(
                                out=acc[:, half, :W], in_=psb[pb][:, hl:hl + W], func=AF.Identity,
                                bias=vcol(l, "fcb", ch), scale=vcol(l, "fcw", 1 * 44 + ch))),
                                reads=[Rps[pb], R_const], writes=[RACC[ai][half]])
                            lo = 0 if hl else 1
                            P.op("dve", ("scalar_tensor_tensor", dict(
                                out=acc[:, half, lo:W], in0=psb[pb][:, hl - 1 + lo:hl - 1 + W], scalar=vcol(l, "fcw", 0 * 44 + ch),
                                in1=acc[:, half, lo:W], op0=ALU.mult, op1=ALU.add)),
                                reads=[Rps[pb], R_const], writes=[RACC[ai][half]])
                            hi = W if hr else W - 1
                            P.op("dve", ("scalar_tensor_tensor", dict(
                                out=acc[:, half, 0:hi], in0=psb[pb][:, hl + 1:hl + 1 + hi], scalar=vcol(l, "fcw", 2 * 44 + ch),
                                in1=acc[:, half, 0:hi], op0=ALU.mult, op1=ALU.add)),
                                reads=[Rps[pb], R_const], writes=[RACC[ai][half]])
                        gcol = t["gc"]

                        def fin(ai=ai, W=W, acc=acc, j=j, gcol=gcol):
                            P.op("act", ("activation", dict(out=SG[ai][:, :W], in_=acc[:, 0, :W], func=AF.Silu)),
                                 reads=[RACC[ai][0]], writes=[RSG[ai]])
                            P.op("pool", ("tensor_tensor", dict(
                                out=GV[:, j, gcol:gcol + W], in0=SG[ai][:, :W], in1=acc[:, 1, :W], op=ALU.mult)),
                                reads=[RSG[ai], RACC[ai][1]], writes=[RGV[j]])

                        pend.append(fin)
                        if len(pend) > 1:
                            pend.pop(0)()
                while pend:
                    pend.pop(0)()
                if sti == 0 and mt is not None:
                    mt[2][11]()
                    mt[3]()
                    reserved.discard(7)
                nparts = []
                if sti + 1 == len(supers) and nxt is not None:
                    guard = [R["XH"], R["SQH"], R["RTH"]] + RH2
                    P.op("pool", ("dma_start", dict(out=nxt["WIN"], in_=win_d[l + 1], max_dma_last_dim=4096)),
                         writes=[nxt["R_WIN"]] + guard, dma=True)
                    P.op("pool", ("dma_start", dict(out=nxt["WUKV"], in_=wukv_d[l + 1], max_dma_last_dim=4096)),
                         writes=[nxt["R_WUKV"]] + guard, dma=True)
                if sti + 1 < len(supers):
                    for jj in range(3):
                        load_wup(jj)
                    nparts = h2_parts(supers[sti + 1])
                    nparts[0][0]()
                def load_xm(mm):
                    lst = []
                    for i, t in enumerate(ST):
                        W = t["b"] - t["a"]
                        ca = t["t0"] + t["a"]
                        xi = cnt_xm[0] % 6
                        cnt_xm[0] += 1
                        lst.append(xi)
                        P.op("sp", ("dma_start", dict(out=XM[xi][:, :W], in_=xsrc[mm, :, ca:ca + W])),
                             reads=[R_xsrc], writes=[RXM[xi]], dma=True)
                    return lst

                xis_next = load_xm(0)
                for m in range(8):
                    sl = m % 3
                    if m + 2 < 8:
                        load_wdn(m + 2)
                    xis = xis_next
                    if m + 1 < 8:
                        xis_next = load_xm(m + 1)
                    for i, t in enumerate(ST):
                        W = t["b"] - t["a"]
                        gcol = t["gc"]
                        g = t["g"]
                        ca = t["t0"] + t["a"]
                        xi = xis[i]
                        pb = nb()
                        fns = [("matmul", dict(out=psb[pb][:, :W], lhsT=WDN[sl][:, j, :], rhs=GV[:, j, gcol:gcol + W],
                            start=(j == 0), stop=(j == NJ - 1))) for j in range(NJ)]
                        JS = NJ - 4
                        P.op("pe", fns[:JS], reads=[RWDN[sl]] + RGV[:JS], writes=[Rps[pb]])
                        P.op("pe", fns[JS:], reads=[RWDN[sl]] + RGV[JS:], writes=[Rps[pb]])
                        P.op("dve", ("scalar_tensor_tensor", dict(
                            out=XM[xi][:, :W], in0=psb[pb][:, :W], scalar=MOD[:, l, 40 + m, g:g + 1], in1=XM[xi][:, :W],
                            op0=ALU.mult, op1=ALU.add)),
                            reads=[Rps[pb], RXM[xi], Rm()], writes=[RXM[xi]])
                        P.op("sp", ("dma_start", dict(out=xdst[m, :, ca:ca + W], in_=XM[xi][:, :W])),
                             reads=[RXM[xi]], writes=[R_xdst], dma=True)
                    if nparts and m % 2 == 1:
                        pi = m // 2
                        if pi < len(nparts):
                            nparts[pi][1]()
                            if pi + 1 < len(nparts):
                                nparts[pi + 1][0]()
            P.barrier()

        def final_norm(xsrc, R_xsrc):
            AR.off = reg0
            XT = AR.alloc([8, 512], F32)
            SQ = AR.alloc([8, 512], BF16)
            YO = AR.alloc([8, 512], F32)
            RT = AR.alloc([512], F32)
            RX, RS, RY, RR = Res("fx"), Res("fs"), Res("fy"), Res("fr")
            for a in range(0, NTOK, 512):
                W = 512
                P.op("sp", ("dma_start", dict(out=XT[:, :, :W], in_=xview(xsrc)[:, :, a:a + W])),
                     reads=[R_xsrc], writes=[RX], dma=True)
                P.op("act", ("activation", dict(out=SQ[:, :, :W], in_=XT[:, :, :W], func=AF.Square)), reads=[RX], writes=[RS])
                pb = nb()
                fns = [("matmul", dict(out=psb[pb][:, :W], lhsT=ones_b, rhs=SQ[:, k, :W], start=(k == 0), stop=(k == 7)))
                       for k in range(8)]
                P.op("pe", fns, reads=[RS, R_const], writes=[Rps[pb]])
                emit_rstd(pb, W, float(D), RT, RR)
                for k in range(8):
                    P.op("dve", ("scalar_tensor_tensor", dict(
                        out=YO[:, k, :W], in0=XT[:, k, :W], scalar=vecs[:, DEPTH * VPL + k:DEPTH * VPL + k + 1], in1=psb[pb][:, :W],
                        op0=ALU.mult, op1=ALU.mult)),
                        reads=[RX, Rps[pb], R_const], writes=[RY])
                o = P.op("sp", ("dma_start", dict(out=xview(y_d)[:, :, a:a + W], in_=YO[:, :, :W])), reads=[RY], dma=True)
                out_tokens.append(o.tok)

        R_xin, R_X0, R_X1 = Res("xin"), Res("X0"), Res("X1")
        AR.off = reg0
        for shp, dt_ in (([4, 2560], BF16), ([20, 512], BF16), ([2560], BF16), ([3, 2048], BF16), ([2, 2048 + 16], F32),
                         ([2, 2048 + 30], BF16), ([8, 1024], BF16), ([8, 512], F32), ([4, 512], F32), ([4, 512], BF16), ([512], F32)):
            AR.alloc(shp, dt_)
        smp_slot0 = AR.off
        smp_WIN = AR.alloc([8, INW], BF16)
        smp_WUKV = AR.alloc([2, 1024], BF16)
        pre_next = None
        for l in range(depth):
            cur_l[0] = l
            src, Rsrc = (xin, R_xin) if l == 0 else (X0, R_X0)
            mixer_pass(l, sample_seqs, src, X1, Rsrc, R_X1, pre=pre_next)
            pre_next = None
            if l + 1 < depth:
                pre_next = dict(off=smp_slot0, WIN=smp_WIN, WUKV=smp_WUKV, R_WIN=Res("WINp"), R_WUKV=Res("WUKVp"))
            fg = ffn_phase(l, X1, X0, R_X1, R_X0, nxt=pre_next)
            mixer_pass(l, prompt_seqs, src, X1, Rsrc, R_X1, hook=lambda: next(fg))
            for _ in fg:
                pass
        final_norm(X0, R_X0)
        P.emit(final_waits=out_tokens)
    return nc


def _host_layout(inp):
    f32 = np.float32
    D_ = D
    x_prompt = np.asarray(inp["x_prompt"], f32)
    x_sample = np.asarray(inp["x_sample"], f32)
    def fm(v):
        v = np.asarray(v, f32)
        return v.reshape(-1, 128).T

    vecs = np.zeros((128, NV), f32)
    for l in range(DEPTH):
        b = l * VPL
        vecs[:, b + VO["n1g"]:b + VO["n1g"] + 8] = fm(inp["norm1_g"][l])
        vecs[:, b + VO["n2g"]:b + VO["n2g"] + 8] = fm(inp["norm2_g"][l])
        vecs[:, b + VO["adab"]:b + VO["adab"] + 48] = fm(inp["ada_b"][l])
        vecs[:, b + VO["pscale"]:b + VO["pscale"] + 2] = fm(inp["pool_scale"][l])
        vecs[:, b + VO["qg"]:b + VO["qg"] + 3] = fm(inp["q_norm_g"][l])
        vecs[:, b + VO["kvg"]:b + VO["kvg"] + 2] = fm(inp["kv_norm_g"][l])
        cw = np.asarray(inp["conv_w"][l], f32)
        for k in range(31):
            vecs[:, b + VO["convw"] + 2 * k:b + VO["convw"] + 2 * k + 2] = fm(cw[k])
        vecs[:, b + VO["convb"]:b + VO["convb"] + 2] = fm(inp["conv_b"][l])
        vecs[:, b + VO["lng"]:b + VO["lng"] + 2] = fm(inp["conv_ln_g"][l])
        vecs[:, b + VO["lnb"]:b + VO["lnb"] + 2] = fm(inp["conv_ln_b"][l])
        fw = np.asarray(inp["ffn_conv_w"][l], f32)
        for k in range(3):
            vecs[:, b + VO["fcw"] + 44 * k:b + VO["fcw"] + 44 * (k + 1)] = fm(fw[k])
        vecs[:, b + VO["fcb"]:b + VO["fcb"] + 44] = fm(inp["ffn_conv_b"][l])
    vecs[:, DEPTH * VPL:DEPTH * VPL + 8] = fm(inp["final_g"])

    n = 16
    inv = (10000.0 ** (-np.arange(n, dtype=f32) / n)).astype(f32)
    row = np.repeat(np.arange(LS // 64, dtype=f32), 64)
    col = np.tile(np.arange(64, dtype=f32), LS // 64)
    ar = (row[:, None] * inv).astype(f32)
    ac = (col[:, None] * inv).astype(f32)
    cs = np.zeros((128, LS), f32)
    cr, sr, cc, sc = np.cos(ar).T, np.sin(ar).T, np.cos(ac).T, np.sin(ac).T
    cs[0:16], cs[16:32], cs[32:48], cs[48:64] = cr, cr, cc, cc
    cs[64:80], cs[80:96], cs[96:112], cs[112:128] = -sr, sr, -sc, sc
    swap = np.concatenate([np.arange(16, 32), np.arange(0, 16), np.arange(48, 64), np.arange(32, 48)])

    ident = np.eye(128, dtype=f32)
    wins = {(0, 0): 2, (0, 1): 4, (1, 0): 8, (1, 1): 16}
    poolf = np.ones((128, 2, 2, 8), f32)
    invw = np.zeros((128, 2), f32)
    for c in range(2):
        for hf in range(2):
            w = wins[(c, hf)]
            p = slice(hf * 64, hf * 64 + 64)
            invw[p, c] = 1.0 / w
            for t in range(8):
                cl = min(w, t + w // 2)
                poolf[p, 0, c, t] = w / cl
                tr = 8 - t
                cr_ = min(w, tr + w // 2)
                poolf[p, 1, c, t] = w / cr_
    pwbd = np.zeros((DEPTH, 128, 2, 128), f32)
    pw = np.asarray(inp["pool_w"], f32)
    for l in range(DEPTH):
        for c in range(2):
            for hf in range(2):
                gidx = c * 2 + hf
                pwbd[l, hf * 64:hf * 64 + 64, c, hf * 64:hf * 64 + 64] = pw[l, gidx]

    def kmaj(w, kch):
        return np.ascontiguousarray(w.reshape(kch, 128, -1).transpose(1, 0, 2))

    ada = np.zeros((DEPTH, 12, 128, 8, 512), f32)
    win = np.zeros((DEPTH, 128, 8, INW), f32)
    wuq = np.zeros((DEPTH, 128, 3, 1024), f32)
    wukv = np.zeros((DEPTH, 128, 2, 1024), f32)
    wout = np.zeros((DEPTH, 128, 8, 1024), f32)
    wup = np.zeros((DEPTH, NJ, 128, 8, 256), f32)
    wdn = np.zeros((DEPTH, 8, 128, NJ, 128), f32)
    for l in range(DEPTH):
        aw = kmaj(np.asarray(inp["ada_w"][l], f32), 8)
        for blk in range(12):
            ada[l, blk] = aw[:, :, blk * 512:(blk + 1) * 512]
        wi = np.asarray(inp["w_in"][l], f32)
        wi = np.concatenate([wi, wi[:, 896 + swap]], axis=1)
        win[l] = kmaj(wi, 8)
        wq = np.asarray(inp["w_uq"][l], f32).reshape(384, 4, 192)
        wq_n = wq[:, :, :128].reshape(384, 512)
        wq_r = wq[:, :, 128:].reshape(384, 256)
        wq_s = wq[:, :, 128:][:, :, swap].reshape(384, 256)
        wuq[l] = kmaj(np.concatenate([wq_n, wq_r, wq_s], axis=1), 3)
        wk = np.asarray(inp["w_ukv"][l], f32).reshape(256, 4, 256)
        wukv[l] = kmaj(np.concatenate([wk[:, :, :128].reshape(256, 512), wk[:, :, 128:].reshape(256, 512)], axis=1), 2)
        wout[l] = kmaj(np.asarray(inp["w_out"][l], f32), 8)
        wu = kmaj(np.asarray(inp["w_up"][l], f32), 8)
        for j in range(NJ):
            wup[l, j, :, :, :128] = wu[:, :, j * 128:(j + 1) * 128]
            wup[l, j, :, :, 128:] = wu[:, :, DFF + j * 128:DFF + (j + 1) * 128]
        wd = kmaj(np.asarray(inp["w_down"][l], f32), NJ)
        for m in range(8):
            wdn[l, m] = wd[:, :, m * 128:(m + 1) * 128]

    cdiag = np.zeros((DEPTH, 128, 62, 128), f32)
    pidx = np.arange(128)
    for l in range(DEPTH):
        cw = np.asarray(inp["conv_w"][l], f32)
        for c in range(2):
            for k in range(31):
                cdiag[l, pidx, c * 31 + k, pidx] = cw[k, c * 128:(c + 1) * 128]
    shared = dict(cdiag=cdiag, vecs=vecs, cs=cs, ident=ident, poolf=poolf, invw=invw, pwbd=pwbd, ada=ada, win=win, wuq=wuq,
                  wukv=wukv, wout=wout, wup=wup, wdn=wdn)
    maps = []
    cckv = np.asarray(inp["cache_ckv"], f32)
    ckr = np.asarray(inp["cache_krope"], f32)
    cvec = np.asarray(inp["c"], f32)
    cctx = np.asarray(inp["c_ctx"], f32)
    for c in range(NCORES):
        X = np.concatenate([x_sample[c], x_prompt[2 * c], x_prompt[2 * c + 1]], axis=0)
        xin = np.ascontiguousarray(X.T.reshape(8, 128, NTOK))
        craw = np.stack([fm(cvec[c]), fm(cctx)], axis=-1)
        ck = np.ascontiguousarray(cckv[c].transpose(0, 2, 1).reshape(DEPTH, 2, 128, PAST).transpose(0, 2, 1, 3))
        kr = np.ascontiguousarray(ckr[c].transpose(0, 2, 1))
        m = dict(shared)
        m.update(xin=xin, craw=np.ascontiguousarray(craw), cckv=ck, ckr=kr)
        maps.append(m)
    return maps


_NC_CACHE = {}


def kernel(**inputs):
    maps = _host_layout(inputs)
    if "nc" not in _NC_CACHE:
        _NC_CACHE["nc"] = build()
    nc = _NC_CACHE["nc"]
    res = run_bass_kernel_spmd(nc, maps, core_ids=list(range(NCORES)))
    y_prompt = np.zeros((16, LP, D), np.float32)
    y_sample = np.zeros((8, LS, D), np.float32)
    s_ckv = np.zeros((16, DEPTH, LP, 256), np.float32)
    s_kr = np.zeros((16, DEPTH, LP, 64), np.float32)
    for c in range(NCORES):
        r = res.results[c]
        Y = np.asarray(r["y"]).reshape(D, NTOK).T
        y_sample[c] = Y[:LS]
        y_prompt[2 * c] = Y[LS:LS + LP]
        y_prompt[2 * c + 1] = Y[LS + LP:]
        ck = np.asarray(r["sckv"])
        kr = np.asarray(r["skr"])
        for s in range(2):
            s_ckv[2 * c + s] = ck[:, s].reshape(DEPTH, 256, LP).transpose(0, 2, 1)
            s_kr[2 * c + s] = kr[:, s].transpose(0, 2, 1)
    return (y_prompt, y_sample, s_ckv, s_kr)
```

```python
import math
from contextlib import ExitStack

import numpy as np
import concourse.bass as bass
import concourse.mybir as mybir
from concourse.bass_utils import run_bass_kernel_spmd

F32 = mybir.dt.float32
BF16 = mybir.dt.bfloat16
U8 = mybir.dt.uint8
AF = mybir.ActivationFunctionType
ALU = mybir.AluOpType

NCORES = 8
D = 1024
DEPTH = 4
LS = 2048
LP = 256
NTOK = LS + 2 * LP
PAST = 512
DFF = 2816
NJ = DFF // 128
EPS = 1e-6
INW = 1536
MODS_INTERLEAVE = True
ATT_SCALE = 1.0 / math.sqrt(192.0)

VO = {}
_o = 0
for _n, _w in (("n1g", 8), ("n2g", 8), ("adab", 48), ("pscale", 2), ("qg", 3), ("kvg", 2),
               ("convw", 62), ("convb", 2), ("lng", 2), ("lnb", 2), ("fcw", 132), ("fcb", 44)):
    VO[_n] = _o
    _o += _w
VPL = _o
NV = VPL * DEPTH + 8

ENGS = ("pe", "act", "dve", "pool", "sp")
NDMA = 8


class Res:
    __slots__ = ("name", "w", "rs", "rd")

    def __init__(self, name):
        self.name = name
        self.w = None
        self.rs = {}
        self.rd = []


class _Op:
    __slots__ = ("eng", "fns", "waits", "sig", "dma", "tok", "pre")

    def __init__(self, eng, fns, dma):
        self.eng = eng
        self.fns = fns
        self.waits = []
        self.sig = False
        self.dma = dma
        self.tok = None
        self.pre = None


class Prog:
    def __init__(self, nc, stack):
        self.nc = nc
        self.ops = []
        self.sems = {e: stack.enter_context(nc.semaphore("s_" + e)) for e in ENGS}
        self.dsems = {e: [stack.enter_context(nc.semaphore(f"d_{e}{i}")) for i in range(NDMA)]
                      for e in ("sp", "pool")}
        self.dcnt = {e: [0] * NDMA for e in self.dsems}
        self.dnext = {e: 0 for e in self.dsems}
        self.cnt = {e: 0 for e in ENGS}
        self.waited = {e: {} for e in ENGS}
        self.last = {e: None for e in ENGS}
        self.dmas = []

    def op(self, eng, fns, reads=(), writes=(), dma=False):
        if callable(fns) or isinstance(fns, tuple):
            fns = [fns]
        fns = [(lambda e, m=f[0], kw=f[1]: getattr(e, m)(**kw)) if isinstance(f, tuple) else f for f in fns]
        o = _Op(eng, list(fns), dma)
        o.tok = ["pending", None, o]

        def _flat(xs):
            out = []
            for x in xs:
                if isinstance(x, (list, tuple)):
                    out.extend(x)
                else:
                    out.append(x)
            return out
        reads = _flat(reads)
        writes = _flat(writes)
        toks = []
        for r in reads:
            toks.append(r.w)
        for r in writes:
            toks.append(r.w)
            toks.extend(r.rs.values())
            toks.extend(r.rd)
        for t in toks:
            if t is not None and t[2] is not o:
                if eng == "pe" and t[2].eng == "pe" and not t[2].dma:
                    continue
                o.waits.append(t)
                t[2].sig = True
        for r in reads:
            if dma:
                r.rd.append(o.tok)
            else:
                r.rs[eng] = o.tok
        for r in writes:
            r.w = o.tok
            r.rs = {}
            r.rd = []
        self.ops.append(o)
        if dma:
            self.dmas.append(o.tok)
        else:
            self.last[eng] = o
        return o

    def barrier(self):
        toks = []
        for e in ENGS:
            lo = self.last[e]
            if lo is not None:
                lo.sig = True
                toks.append(lo.tok)
        toks.extend(self.dmas)
        self.dmas = []
        for e in ENGS:
            o = _Op(e, [], False)
            o.tok = ["pending", None, o]
            o.waits = list(toks)
            self.ops.append(o)

    def emit(self, final_waits=()):
        nc = self.nc
        for o in self.ops:
            if o.dma:
                q = o.eng
                i = self.dnext[q]
                self.dnext[q] = (i + 1) % NDMA
                prev = self.dcnt[q][i]
                o.pre = (("d", q, i), prev) if prev > 0 else None
                self.dcnt[q][i] += 16
                o.tok[0] = ("d", q, i)
                o.tok[1] = self.dcnt[q][i]
                o.sig = True
            elif o.sig:
                self.cnt[o.eng] += 1
                o.tok[0] = ("e", o.eng)
                o.tok[1] = self.cnt[o.eng]
        per = {e: [] for e in ENGS}
        for o in self.ops:
            per[o.eng].append(o)

        def semof(key):
            if key[0] == "e":
                return self.sems[key[1]]
            return self.dsems[key[1]][key[2]]

        def run(e, handle):
            waited = self.waited[e]
            for o in per[e]:
                need = {}
                if o.pre is not None:
                    need[o.pre[0]] = o.pre[1]
                for t in o.waits:
                    k, v = t[0], t[1]
                    if need.get(k, 0) < v:
                        need[k] = v
                for k, v in need.items():
                    if waited.get(k, 0) < v:
                        handle.wait_ge(semof(k), v)
                        waited[k] = v
                ins = None
                for f in o.fns:
                    ins = f(handle)
                if o.sig and ins is not None:
                    if o.dma:
                        ins.then_inc(semof(o.tok[0]), 16)
                    else:
                        ins.then_inc(self.sems[e], 1)
            if e == "sp":
                for t in final_waits:
                    k, v = t[0], t[1]
                    if waited.get(k, 0) < v:
                        handle.wait_ge(semof(k), v)
                        waited[k] = v

        with nc.Block() as block:
            @block.tensor
            def _(h):
                run("pe", h)

            @block.scalar
            def _(h):
                run("act", h)

            @block.vector
            def _(h):
                run("dve", h)

            @block.gpsimd
            def _(h):
                run("pool", h)

            @block.sync
            def _(h):
                run("sp", h)


class Arena:
    def __init__(self, ap, size):
        self.ap = ap
        self.size = size
        self.off = 0

    def alloc(self, shape, dt, part=128):
        esz = 4 if dt == F32 else 2
        n = 1
        for s in shape:
            n *= s
        nb = (n * esz + 63) // 64 * 64
        assert self.off + nb <= self.size, f"arena overflow {self.off}+{nb}>{self.size}"
        v = self.ap[:, self.off:self.off + n * esz].bitcast(dt)
        self.off += nb
        if len(shape) == 2:
            v = v.rearrange("p (a b) -> p a b", b=shape[1])
        elif len(shape) == 3:
            v = v.rearrange("p (a b c) -> p a b c", b=shape[1], c=shape[2])
        return v


def build(depth=DEPTH):
    nc = bass.Bass("TRN2", target_bir_lowering=False, dynamic_dma_scratch_size=4096)

    def din(name, shape):
        return nc.dram_tensor(name, list(shape), F32, kind="ExternalInput").ap()

    xin = din("xin", (8, 128, NTOK))
    vecs_d = din("vecs", (128, NV))
    cs_d = din("cs", (128, LS))
    ident_d = din("ident", (128, 128))
    craw_d = din("craw", (128, 8, 2))
    poolf_d = din("poolf", (128, 2, 2, 8))
    invw_d = din("invw", (128, 2))
    pwbd_d = din("pwbd", (DEPTH, 128, 2, 128))
    ada_d = din("ada", (DEPTH, 12, 128, 8, 512))
    win_d = din("win", (DEPTH, 128, 8, INW))
    wuq_d = din("wuq", (DEPTH, 128, 3, 1024))
    wukv_d = din("wukv", (DEPTH, 128, 2, 1024))
    wout_d = din("wout", (DEPTH, 128, 8, 1024))
    wup_d = din("wup", (DEPTH, NJ, 128, 8, 256))
    wdn_d = din("wdn", (DEPTH, 8, 128, NJ, 128))
    cckv_d = din("cckv", (DEPTH, 128, 2, PAST))
    ckr_d = din("ckr", (DEPTH, 64, PAST))
    cdiag_d = din("cdiag", (DEPTH, 128, 62, 128))

    y_d = nc.dram_tensor("y", [8, 128, NTOK], F32, kind="ExternalOutput").ap()
    sckv_d = nc.dram_tensor("sckv", [DEPTH, 2, 2, 128, LP], F32, kind="ExternalOutput").ap()
    skr_d = nc.dram_tensor("skr", [DEPTH, 2, 64, LP], F32, kind="ExternalOutput").ap()
    X0 = nc.dram_tensor("xs0", [8, 128, NTOK], F32).ap()
    X1 = nc.dram_tensor("xs1", [8, 128, NTOK], F32).ap()

    def xview(t):
        return t.rearrange("k p t -> p k t")

    with ExitStack() as st:
        ASZ = 188 * 1024
        arena_t = st.enter_context(nc.sbuf_tensor("arena", [128, ASZ], U8))
        AR = Arena(arena_t, ASZ)
        psb = [st.enter_context(nc.psum_tensor(f"ps{i}", [128, 512], F32)) for i in range(8)]
        Rph = [Res(f"ph{i}") for i in range(16)]
        Rps = [[Rph[2 * i], Rph[2 * i + 1]] for i in range(8)]
        hrot = [0]

        def nbh():
            b = nb()
            return psb[b][:, 0:256], Rps[b], b
        P = Prog(nc, st)
        rot = [0]
        reserved = set()

        def nb():
            while True:
                i = rot[0]
                rot[0] = (i + 1) % 8
                if i not in reserved:
                    return i

        vecs = AR.alloc([NV], F32)
        cs = AR.alloc([LS], F32)
        ident_f = AR.alloc([128], F32)
        ones_f = AR.alloc([128], F32)
        ones_b = AR.alloc([128], BF16)
        craw = AR.alloc([8, 2], F32)
        csil = AR.alloc([8, 2], F32)
        MOD = AR.alloc([DEPTH, 48, 2], F32)
        AMOD = AR.alloc([DEPTH * 2, 8, 2], F32)
        poolf = AR.alloc([2, 2, 8], F32)
        invw = AR.alloc([2], F32)
        pwbd = AR.alloc([2, 128], BF16)
        R_const = Res("const")
        R_mods = [Res(f"mod{i}") for i in range(DEPTH)]
        cur_l = [0]

        def Rm():
            return R_mods[cur_l[0]]
        R_pwbd = Res("pwbd")
        const_end = AR.off

        P.op("sp", ("dma_start", dict(out=vecs, in_=vecs_d)), writes=[R_const], dma=True)
        P.op("sp", ("dma_start", dict(out=cs, in_=cs_d)), writes=[R_const], dma=True)
        P.op("sp", ("dma_start", dict(out=ident_f, in_=ident_d)), writes=[R_const], dma=True)
        P.op("sp", ("dma_start", dict(out=craw, in_=craw_d)), writes=[R_const], dma=True)
        P.op("sp", ("dma_start", dict(out=poolf, in_=poolf_d)), writes=[R_const], dma=True)
        P.op("sp", ("dma_start", dict(out=invw, in_=invw_d)), writes=[R_const], dma=True)
        P.op("dve", ("memset", dict(ap=ones_f, constant=1.0)), writes=[R_const])
        P.op("dve", ("memset", dict(ap=ones_b, constant=1.0)), writes=[R_const])
        P.op("act", ("activation", dict(out=csil, in_=craw, func=AF.Silu)), reads=[R_const], writes=[R_const])

        def vcol(l, name, i, n=1, p0=0, p1=128):
            c = l * VPL + VO[name] + i
            return vecs[p0:p1, c:c + n]

        csil_b = AR.alloc([8, 2], BF16)
        P.op("dve", ("tensor_copy", dict(out=csil_b, in_=csil)), reads=[R_const], writes=[R_const])
        epsc = AR.alloc([2], F32)
        P.op("dve", ("memset", dict(ap=epsc, constant=EPS)), writes=[R_const])
        reg0 = AR.off
        FFN_HI = reg0 + 104 * 1024

        def mod_tasks(l, stg, Rstg, rowb, Rrow, pb):
            pst = psb[pb][:, 0:96].rearrange("p (q g) -> p q g", g=2)
            ns = len(stg)

            def mk_load(blk):
                def f():
                    P.op("pool", ("dma_start", dict(out=stg[blk % ns], in_=ada_d[l, blk], max_dma_last_dim=4096)),
                         writes=[Rstg[blk % ns]], dma=True)
                return f

            def mk_block(blk):
                def f():
                    sidx = blk % ns
                    r = blk % 2
                    pr = nb()
                    fns = [("matmul", dict(out=psb[pr][0:2, :], lhsT=csil_b[:, k, :], rhs=stg[sidx][:, k, :],
                                           start=(k == 0), stop=(k == 7))) for k in range(8)]
                    P.op("pe", fns, reads=[Rstg[sidx], R_const], writes=[Rps[pr]])
                    P.op("act", ("activation", dict(out=rowb[r][0:2, :], in_=psb[pr][0:2, :], func=AF.Copy)),
                         reads=[Rps[pr]], writes=[Rrow[r]])
                return f

            def mk_trans(blk):
                def f():
                    r = blk % 2
                    fns = [("matmul", dict(out=pst[:, blk * 4 + i, :], lhsT=rowb[r][0:2, i * 128:(i + 1) * 128],
                                           rhs=ident_f[0:2, 0:2], start=True, stop=True)) for i in range(4)]
                    P.op("pe", fns, reads=[Rrow[r], R_const], writes=[Rps[pb]])
                return f

            def final():
                for g in range(2):
                    P.op("dve", ("tensor_tensor", dict(
                        out=MOD[:, l, :, g], in0=pst[:, :, g], in1=vcol(l, "adab", 0, 48), op=ALU.add)),
                        reads=[Rps[pb], R_const], writes=[R_mods[l]])
                for which, (sc0, gn) in enumerate(((8, "n1g"), (32, "n2g"))):
                    for g in range(2):
                        P.op("dve", ("scalar_tensor_tensor", dict(
                            out=AMOD[:, l * 2 + which, :, g], in0=MOD[:, l, sc0:sc0 + 8, g], scalar=1.0,
                            in1=vcol(l, gn, 0, 8), op0=ALU.add, op1=ALU.mult)),
                            reads=[R_mods[l], R_const], writes=[R_mods[l]])
            return ([mk_load(b) for b in range(12)], [mk_block(b) for b in range(12)],
                    [mk_trans(b) for b in range(12)], final)

        def emit_mods():
            AR.off = reg0
            stg = [AR.alloc([8, 512], BF16) for _ in range(4)]
            rowb = [AR.alloc([512], F32) for _ in range(2)]
            Rstg = [Res(f"stg{i}") for i in range(4)]
            Rrow = [Res("row0"), Res("row1")]
            pb = 7
            reserved.add(pb)
            for ll in (range(1) if MODS_INTERLEAVE else range(depth)):
                loads, blocks, trans, final = mod_tasks(ll, stg, Rstg, rowb, Rrow, pb)
                for b in range(3):
                    loads[b]()
                for b in range(12):
                    blocks[b]()
                    if b + 3 < 12:
                        loads[b + 3]()
                    if b >= 1:
                        trans[b - 1]()
                trans[11]()
                final()
            reserved.discard(pb)

        emit_mods()
        P.barrier()

        def emit_rstd(pb, W, n, rtmp, R_rtmp):
            P.op("act", ("activation", dict(out=rtmp[:, :W], in_=psb[pb][:, :W], func=AF.Sqrt,
                                               bias=epsc[:, 0:1], scale=1.0 / n)),
                 reads=[Rps[pb], R_const], writes=[R_rtmp])
            P.op("dve", ("reciprocal", dict(out=psb[pb][:, :W], in_=rtmp[:, :W])),
                 reads=[R_rtmp], writes=[Rps[pb]])


        def emit_norm_a(xt, R_x, W, sq, R_sq):
            P.op("act", ("activation", dict(out=sq[:, :, :W], in_=xt[:, :, :W], func=AF.Square)),
                 reads=[R_x], writes=[R_sq])

        def emit_norm_b(xt, R_x, W, Acol, Bcol, hout, R_h, sq, R_sq, rtmp, R_rtmp):
            pb = nb()
            fns = [("matmul", dict(out=psb[pb][:, :W], lhsT=ones_b, rhs=sq[:, k, :W],
                                   start=(k == 0), stop=(k == 7))) for k in range(8)]
            P.op("pe", fns, reads=[R_sq, R_const], writes=[Rps[pb]])
            emit_rstd(pb, W, float(D), rtmp, R_rtmp)
            rb = psb[pb][:, :W].unsqueeze(1).to_broadcast([128, 8, W])
            P.op("dve", ("tensor_tensor", dict(out=xt[:, :, :W], in0=xt[:, :, :W], in1=rb, op=ALU.mult)),
                 reads=[Rps[pb], R_x], writes=[R_x])
            for k in range(8):
                P.op("act", ("activation", dict(out=hout[:, k, :W], in_=xt[:, k, :W], func=AF.Identity,
                                                bias=Bcol(k), scale=Acol(k))),
                     reads=[R_x, Rm(), R_const], writes=[R_h])

        def emit_norm_mod(xt, R_x, W, Acol, Bcol, gsel, hout, R_h, sq, R_sq, rtmp, R_rtmp):
            emit_norm_a(xt, R_x, W, sq, R_sq)
            emit_norm_b(xt, R_x, W, Acol, Bcol, hout, R_h, sq, R_sq, rtmp, R_rtmp)

        sample_seqs = [dict(t0=0, L=LS, rope=True, ctx=True, g=0, oi=None)]
        prompt_seqs = [dict(t0=LS, L=LP, rope=False, ctx=False, g=1, oi=0),
                       dict(t0=LS + LP, L=LP, rope=False, ctx=False, g=1, oi=1)]

        out_tokens = []

        def mixer_pass(l, seqs, xsrc, xdst, R_xsrc, R_xdst, hook=None, pre=None):
            AR.off = reg0
            nkeys = sum(s["L"] + (PAST if s["ctx"] else 0) for s in seqs)
            ntok = sum(s["L"] for s in seqs)
            nseq = len(seqs)
            WOUT = AR.alloc([8, 1024], BF16)
            KN = AR.alloc([4, nkeys], BF16)
            VV = AR.alloc([nkeys // 128, 512], BF16)
            KR = AR.alloc([nkeys], BF16)
            QN = AR.alloc([3, ntok], BF16)
            ZP = AR.alloc([2, ntok + 16 * nseq], F32)
            AG = AR.alloc([2, ntok + 30 * nseq], BF16)
            XT = AR.alloc([8, 512], F32)
            SCR = AR.alloc([4, 512], F32)
            SQ = SCR.rearrange("p a b -> p (a b)").bitcast(BF16).rearrange("p (a b) -> p a b", b=512)
            YPC = AR.alloc([4, 512], BF16)
            RT = AR.alloc([512], F32)
            slot0 = AR.off
            WIN = AR.alloc([8, INW], BF16)
            WUKV = AR.alloc([2, 1024], BF16)
            HH = AR.alloc([8, 512], BF16)
            endA = AR.off
            AR.off = slot0
            DIAG = AR.alloc([62, 128], BF16)
            QNP = AR.alloc([4, 512], BF16)
            QRP = AR.alloc([4, 512], BF16)
            YA = AR.alloc([4, 512], BF16)
            PT = [AR.alloc([512], BF16) for _ in range(4)]
            WUQ = AR.alloc([3, 1024], BF16)
            if hook is not None:
                assert max(endA, AR.off) <= FFN_HI, (endA, AR.off, FFN_HI)
            R = {n: Res(n) for n in ("KN", "VV", "KR", "QN", "ZP", "AG", "WOUT", "XT", "SCR", "YPC", "RT",
                                     "WIN", "WUKV", "HH", "DIAG", "QNP", "QRP", "YA", "WUQ", "CK", "DD")}
            RPT = [Res(f"PT{i}") for i in range(4)]
            RYA = [Res(f"YA{i}") for i in range(4)]
            RQN = [Res(f"QNP{i}") for i in range(4)]
            RQR = [Res(f"QRP{i}") for i in range(4)]
            R["D32"] = Res("D32")
            R["U32"] = Res("U32")
            g = seqs[0]["g"]

            kc = 0
            zc = 0
            ac = 0
            qc = 0
            for s in seqs:
                s["kc"] = kc
                kc += s["L"] + (PAST if s["ctx"] else 0)
                s["zc"] = zc
                zc += s["L"] + 16
                s["ac"] = ac
                ac += s["L"] + 30
                s["qc"] = qc
                qc += s["L"]

            if pre is not None:
                assert pre["off"] == slot0, (pre["off"], slot0)
                R["WIN"], R["WUKV"] = pre["R_WIN"], pre["R_WUKV"]
            else:
                P.op("pool", ("dma_start", dict(out=WIN, in_=win_d[l], max_dma_last_dim=4096)), writes=[R["WIN"]], dma=True)
                P.op("pool", ("dma_start", dict(out=WUKV, in_=wukv_d[l], max_dma_last_dim=4096)), writes=[R["WUKV"]], dma=True)
            if hook is None:
                P.op("pool", ("dma_start", dict(out=WOUT, in_=wout_d[l], max_dma_last_dim=4096)), writes=[R["WOUT"]], dma=True)
                P.op("pool", ("dma_start", dict(out=pwbd, in_=pwbd_d[l])), writes=[R_pwbd], dma=True)
            P.op("pool", ("memset", dict(ap=ZP, constant=0.0)), writes=[R["ZP"]])
            P.op("pool", ("memset", dict(ap=AG, constant=0.0)), writes=[R["AG"]])
            if hook is not None:
                hook()

            CK = SCR

            def kv_project(ckb, W, keycol, R_cks):
                for h in range(4):
                    pb = nb()
                    fns = [("matmul", dict(out=psb[pb][:, :W], lhsT=WUKV[:, k, h * 128:(h + 1) * 128],
                                           rhs=ckb[:, k, :W], start=(k == 0), stop=(k == 1))) for k in range(2)]
                    P.op("pe", fns, reads=[R["WUKV"]] + R_cks, writes=[Rps[pb]])
                    P.op("act", ("activation", dict(out=KN[:, h, keycol:keycol + W], in_=psb[pb][:, :W], func=AF.Copy)),
                         reads=[Rps[pb]], writes=[R["KN"]])
                for i in range(W // 128):
                    pb = nb()
                    fns = [("matmul", dict(out=psb[pb][:, :], lhsT=ckb[:, k, i * 128:(i + 1) * 128],
                                           rhs=WUKV[:, k, 512:1024], start=(k == 0), stop=(k == 1))) for k in range(2)]
                    P.op("pe", fns, reads=[R["WUKV"]] + R_cks, writes=[Rps[pb]])
                    kt = keycol // 128 + i
                    P.op("dve", ("tensor_copy", dict(out=VV[:, kt, :], in_=psb[pb][:, :])),
                         reads=[Rps[pb]], writes=[R["VV"]])

            RXT = [Res("XT0"), Res("XT1")]
            RHH = [Res("HH0"), Res("HH1")]
            RSC = [Res("SC0"), Res("SC1")]
            RRT = [Res("RT0"), Res("RT1")]
            RYP = [Res("YP0"), Res("YP1")]
            SCRf = SCR.rearrange("p a b -> p (a b)")
            P.op("pool", ("memset", dict(ap=KR[64:128, :], constant=0.0)), writes=[R["KR"]])

            for s in seqs:
                if s["ctx"]:
                    ckc = HH[:, 0:2, :]
                    kcol = s["kc"] + s["L"]
                    P.op("pool", ("dma_start", dict(out=ckc, in_=cckv_d[l], max_dma_last_dim=4096)), writes=RHH, dma=True)
                    P.op("pool", ("dma_start", dict(out=KR[0:64, kcol:kcol + PAST], in_=ckr_d[l])),
                         writes=[R["KR"]], dma=True)
                    kv_project(ckc, PAST, kcol, RHH)

            def tileA(s, a, par):
                W = 256
                L = s["L"]
                ta = s["t0"] + a
                XTp = XT[:, :, par * 256:(par + 1) * 256]
                HHp = HH[:, :, par * 256:(par + 1) * 256]
                SCp = SCRf[:, par * 1024:(par + 1) * 1024]
                SQp = SCp.bitcast(BF16).rearrange("p (a b) -> p a b", b=256)
                CKF = SCp[:, 512:1024].rearrange("p (a b) -> p a b", b=256)
                T1 = SCp[0:64, 512:768]
                T2 = SCp[0:64, 768:1024]
                RTp = RT[:, par * 256:(par + 1) * 256]
                CKB = YPC[:, 2 * par:2 * par + 2, 0:256]
                R_x, R_h, R_s, R_r, R_y = RXT[par], RHH[par], RSC[par], RRT[par], RYP[par]

                def norm_fn():
                    P.op("sp", ("dma_start", dict(out=XTp, in_=xview(xsrc)[:, :, ta:ta + W])),
                         reads=[R_xsrc], writes=[R_x], dma=True)
                    emit_norm_a(XTp, R_x, W, SQp, R_s)
                    yield
                    pb = nb()
                    reserved.add(pb)
                    fns = [("matmul", dict(out=psb[pb][:, :W], lhsT=ones_b, rhs=SQp[:, k, :W],
                                           start=(k == 0), stop=(k == 7))) for k in range(8)]
                    P.op("pe", fns, reads=[R_s, R_const], writes=[Rps[pb]])
                    yield
                    P.op("act", ("activation", dict(out=RTp, in_=psb[pb][:, :W], func=AF.Sqrt,
                                                    bias=epsc[:, 0:1], scale=1.0 / D)),
                         reads=[Rps[pb], R_const], writes=[R_r])
                    yield
                    P.op("dve", ("reciprocal", dict(out=psb[pb][:, :W], in_=RTp)), reads=[R_r], writes=[Rps[pb]])
                    yield
                    rb = psb[pb][:, :W].unsqueeze(1).to_broadcast([128, 8, W])
                    P.op("dve", ("tensor_tensor", dict(out=XTp, in0=XTp, in1=rb, op=ALU.mult)),
                         reads=[Rps[pb], R_x], writes=[R_x])
                    reserved.discard(pb)
                    yield
                    for k in range(8):
                        P.op("act", ("activation", dict(out=HHp[:, k, :], in_=XTp[:, k, :], func=AF.Identity,
                                                        bias=MOD[:, l, 0 + k, g:g + 1], scale=AMOD[:, l * 2 + 0, k, g:g + 1])),
                             reads=[R_x, Rm(), R_const], writes=[R_h])
                        if k % 2 == 1:
                            yield

                def z_fn():
                    def zmm(c0, M, pb):
                        fns = [("matmul", dict(out=pb[0][0:M, :], lhsT=WIN[:, k, c0:c0 + M], rhs=HHp[:, k, :],
                                               start=(k == 0), stop=(k == 7))) for k in range(8)]
                        P.op("pe", fns, reads=[R["WIN"], R_h], writes=[pb[1]])

                    for c in range(2):
                        pb = nbh()
                        zmm(c * 128, 128, pb)
                        zo = s["zc"] + 8 + a
                        P.op("act", ("activation", dict(out=ZP[:, c, zo:zo + W], in_=pb[0], func=AF.Copy)),
                             reads=[pb[1]], writes=[R["ZP"]])
                        yield

                    def latent_norm(c0, nch, gname, outs):
                        pbs = []
                        for c in range(nch):
                            pb = nbh()
                            reserved.add(pb[2])
                            pbs.append(pb)
                            zmm(c0 + c * 128, 128, pb)
                            P.op("act", ("activation", dict(out=SQp[:, c, :], in_=pb[0], func=AF.Square)),
                                 reads=[pb[1]], writes=[R_s])
                            yield
                        ps_s = nbh()
                        fns = [("matmul", dict(out=ps_s[0], lhsT=ones_b, rhs=SQp[:, c, :],
                                               start=(c == 0), stop=(c == nch - 1))) for c in range(nch)]
                        P.op("pe", fns, reads=[R_s, R_const], writes=[ps_s[1]])
                        P.op("act", ("activation", dict(out=RTp, in_=ps_s[0], func=AF.Sqrt,
                                                        bias=epsc[:, 0:1], scale=1.0 / (nch * 128))),
                             reads=[ps_s[1], R_const], writes=[R_r])
                        P.op("dve", ("reciprocal", dict(out=RTp, in_=RTp)), reads=[R_r], writes=[R_r])
                        yield
                        for c in range(nch):
                            for (oap, Ro) in outs(c):
                                P.op("dve", ("scalar_tensor_tensor", dict(
                                    out=oap, in0=pbs[c][0], scalar=vcol(l, gname, c), in1=RTp,
                                    op0=ALU.mult, op1=ALU.mult)),
                                    reads=[pbs[c][1], R_r, R_const], writes=[Ro])
                            reserved.discard(pbs[c][2])
                        yield

                    qo = s["qc"] + a
                    yield from latent_norm(256, 3, "qg", lambda c: [(QN[:, c, qo:qo + W], R["QN"])])
                    if s["oi"] is not None:
                        yield from latent_norm(640, 2, "kvg", lambda c: [(CKB[:, c, :], R_y), (CKF[:, c, :], R_s)])
                        oi = s["oi"]
                        o = P.op("sp", ("dma_start", dict(
                            out=sckv_d[l, oi].rearrange("k p t -> p k t")[:, :, a:a + W], in_=CKF)),
                            reads=[R_s], dma=True)
                        out_tokens.append(o.tok)
                    else:
                        yield from latent_norm(640, 2, "kvg", lambda c: [(CKB[:, c, :], R_y)])

                    kcol = s["kc"] + a
                    pb = nbh()
                    zmm(896, 64, pb)
                    if s["rope"]:
                        pb2 = nbh()
                        zmm(1472, 64, pb2)
                        P.op("dve", ("tensor_tensor", dict(out=T1, in0=pb[0][0:64, :], in1=cs[0:64, a:a + W], op=ALU.mult)),
                             reads=[pb[1], R_const], writes=[R_s])
                        P.op("dve", ("tensor_tensor", dict(out=T2, in0=pb2[0][0:64, :], in1=cs[64:128, a:a + W], op=ALU.mult)),
                             reads=[pb2[1], R_const], writes=[R_s])
                        P.op("dve", ("tensor_tensor", dict(out=KR[0:64, kcol:kcol + W], in0=T1, in1=T2, op=ALU.add)),
                             reads=[R_s], writes=[R["KR"]])
                    else:
                        P.op("act", ("activation", dict(out=KR[0:64, kcol:kcol + W], in_=pb[0][0:64, :], func=AF.Copy)),
                             reads=[pb[1]], writes=[R["KR"]])
                        P.op("act", ("activation", dict(out=T1, in_=pb[0][0:64, :], func=AF.Copy)),
                             reads=[pb[1]], writes=[R_s])
                        oi = s["oi"]
                        o = P.op("sp", ("dma_start", dict(out=skr_d[l, oi][:, a:a + W], in_=T1)), reads=[R_s], dma=True)
                        out_tokens.append(o.tok)
                    yield

                    for c in range(2):
                        pbg = nbh()
                        zmm(1216 + c * 128, 128, pbg)
                        P.op("act", ("activation", dict(out=RTp, in_=pbg[0], func=AF.Sigmoid)),
                             reads=[pbg[1]], writes=[R_r])
                        pba = nbh()
                        zmm(960 + c * 128, 128, pba)
                        ao = s["ac"] + 15 + a
                        P.op("dve", ("tensor_tensor", dict(out=AG[:, c, ao:ao + W], in0=pba[0], in1=RTp, op=ALU.mult)),
                             reads=[pba[1], R_r], writes=[R["AG"]])
                        yield
                    yield from kv_project_g(CKB, W, s["kc"] + a, [R_y])

                return norm_fn, z_fn

            def kv_project_g(ckb, W, keycol, R_cks):
                for h in range(4):
                    pb = nb()
                    fns = [("matmul", dict(out=psb[pb][:, :W], lhsT=WUKV[:, k, h * 128:(h + 1) * 128],
                                           rhs=ckb[:, k, :W], start=(k == 0), stop=(k == 1))) for k in range(2)]
                    P.op("pe", fns, reads=[R["WUKV"]] + R_cks, writes=[Rps[pb]])
                    P.op("act", ("activation", dict(out=KN[:, h, keycol:keycol + W], in_=psb[pb][:, :W], func=AF.Copy)),
                         reads=[Rps[pb]], writes=[R["KN"]])
                    yield
                for i in range(W // 128):
                    pb = nb()
                    fns = [("matmul", dict(out=psb[pb][:, :], lhsT=ckb[:, k, i * 128:(i + 1) * 128],
                                           rhs=WUKV[:, k, 512:1024], start=(k == 0), stop=(k == 1))) for k in range(2)]
                    P.op("pe", fns, reads=[R["WUKV"]] + R_cks, writes=[Rps[pb]])
                    kt = keycol // 128 + i
                    P.op("dve", ("tensor_copy", dict(out=VV[:, kt, :], in_=psb[pb][:, :])),
                         reads=[Rps[pb]], writes=[R["VV"]])
                    yield

            tilesA = []
            for s in seqs:
                for a in range(0, s["L"], 256):
                    tilesA.append(tileA(s, a, len(tilesA) % 2))
            for _ in tilesA[0][0]():
                pass
            for ti in range(len(tilesA)):
                ng = tilesA[ti + 1][0]() if ti + 1 < len(tilesA) else None
                step = 0
                for _ in tilesA[ti][1]():
                    step += 1
                    if ng is not None and step >= 2:
                        next(ng, None)
                if ng is not None:
                    for _ in ng:
                        pass

            P.barrier()
            P.op("pool", ("dma_start", dict(out=WUQ, in_=wuq_d[l], max_dma_last_dim=4096)), writes=[R["WUQ"]], dma=True)
            P.op("pool", ("dma_start", dict(out=DIAG, in_=cdiag_d[l], max_dma_last_dim=4096)), writes=[R["DIAG"]], dma=True)
            P.op("pool", ("memset", dict(ap=QRP[64:128, :, :], constant=0.0)), writes=RQR)
            D32 = SCR[:, 0:2, :]
            U32 = SCR[:, 2:4, :]
            XTf = XT.rearrange("p a b -> p (a b)")
            TA = XTf[:, 0:1056].rearrange("p (a b) -> p a b", b=528)
            TB = XTf[:, 1056:2112].rearrange("p (a b) -> p a b", b=528)
            TC = XTf[:, 2112:2640]
            TD = XTf[:, 2640:3152]
            DB = YA[:, 2:4, :]
            RD, RU = R["D32"], R["U32"]

            def qproj(s, a):
                W = min(512, s["L"] - a)
                qo = s["qc"] + a
                rot[0] = 0
                saved = set(reserved)
                reserved.update({3, 4, 5, 6})
                RD, RU = R["D32"], R["U32"]
                for h in range(4):
                    pb = nb()
                    fns = [("matmul", dict(out=psb[pb][:, :W], lhsT=WUQ[:, k, h * 128:(h + 1) * 128],
                                           rhs=QN[:, k, qo:qo + W], start=(k == 0), stop=(k == 2))) for k in range(3)]
                    P.op("pe", fns, reads=[R["WUQ"], R["QN"]], writes=[Rps[pb]])
                    P.op("dve", ("tensor_copy", dict(out=QNP[:, h, :W], in_=psb[pb][:, :W])),
                         reads=[Rps[pb]], writes=[RQN[h]])
                    pr = nb()
                    fns = [("matmul", dict(out=psb[pr][0:64, :W], lhsT=WUQ[:, k, 512 + h * 64:512 + (h + 1) * 64],
                                           rhs=QN[:, k, qo:qo + W], start=(k == 0), stop=(k == 2))) for k in range(3)]
                    P.op("pe", fns, reads=[R["WUQ"], R["QN"]], writes=[Rps[pr]])
                    if s["rope"]:
                        pr2 = nb()
                        fns = [("matmul", dict(out=psb[pr2][0:64, :W], lhsT=WUQ[:, k, 768 + h * 64:768 + (h + 1) * 64],
                                               rhs=QN[:, k, qo:qo + W], start=(k == 0), stop=(k == 2))) for k in range(3)]
                        P.op("pe", fns, reads=[R["WUQ"], R["QN"]], writes=[Rps[pr2]])
                        T1 = U32[0:64, 0, :]
                        T2q = U32[0:64, 1, :]
                        P.op("dve", ("tensor_tensor", dict(out=T1[:, :W], in0=psb[pr][0:64, :W], in1=cs[0:64, a:a + W], op=ALU.mult)),
                             reads=[Rps[pr], R_const], writes=[RU])
                        P.op("dve", ("tensor_tensor", dict(out=T2q[:, :W], in0=psb[pr2][0:64, :W], in1=cs[64:128, a:a + W], op=ALU.mult)),
                             reads=[Rps[pr2], R_const], writes=[RU])
                        P.op("dve", ("tensor_tensor", dict(out=QRP[0:64, h, :W], in0=T1[:, :W], in1=T2q[:, :W], op=ALU.add)),
                             reads=[RU], writes=[RQR[h]])
                    else:
                        P.op("dve", ("tensor_copy", dict(out=QRP[0:64, h, :W], in_=psb[pr][0:64, :W])),
                             reads=[Rps[pr]], writes=[RQR[h]])


                reserved.clear()
                reserved.update(saved)

            tilesE = [(s, a) for s in seqs for a in range(0, s["L"], 512)]
            qproj(*tilesE[0])
            for ti, (s, a) in enumerate(tilesE):
                if True:
                    L = s["L"]
                    nkt = (L + (PAST if s["ctx"] else 0)) // 128
                    kbase = s["kc"]
                    W = min(512, L - a)
                    ta = s["t0"] + a
                    zb = s["zc"] + 8 + a
                    ab = s["ac"] + a
                    qo = s["qc"] + a

                    P.op("pool", ("tensor_tensor", dict(out=TA[:, :, 0:W + 14], in0=ZP[:, :, zb - 8:zb + W + 6],
                                                        in1=ZP[:, :, zb - 7:zb + W + 7], op=ALU.add)),
                         reads=[R["ZP"]], writes=[R["XT"]])
                    P.op("pool", ("tensor_tensor", dict(out=TB[:, :, 0:W + 12], in0=TA[:, :, 0:W + 12],
                                                        in1=TA[:, :, 2:W + 14], op=ALU.add)),
                         reads=[R["XT"]], writes=[R["XT"]])
                    P.op("pool", ("tensor_tensor", dict(out=TC[:, 0:W + 8], in0=TB[:, 1, 0:W + 8],
                                                        in1=TB[:, 1, 4:W + 12], op=ALU.add)),
                         reads=[R["XT"]], writes=[R["XT"]])
                    P.op("pool", ("tensor_tensor", dict(out=TD[:, 0:W], in0=TC[:, 0:W], in1=TC[:, 8:W + 8], op=ALU.add)),
                         reads=[R["XT"]], writes=[R["XT"]])
                    srcs = [(TA[0:64, 0, 7:7 + W], 0, 0, 64), (TB[64:128, 0, 6:6 + W], 0, 64, 128),
                            (TC[0:64, 4:4 + W], 1, 0, 64), (TD[64:128, 0:W], 1, 64, 128)]
                    for (sap, c, p0, p1) in srcs:
                        P.op("dve", ("scalar_tensor_tensor", dict(
                            out=D32[p0:p1, c, :W], in0=sap, scalar=invw[p0:p1, c:c + 1], in1=ZP[p0:p1, c, zb:zb + W],
                            op0=ALU.mult, op1=ALU.subtract)),
                            reads=[R["XT"], R["ZP"], R_const], writes=[RD])
                    for side, cond, c0 in ((0, a == 0, 0), (1, a + W == L, W - 8)):
                        if not cond:
                            continue
                        dsl = D32[:, :, c0:c0 + 8]
                        zsl = ZP[:, :, zb + c0:zb + c0 + 8]
                        P.op("dve", ("tensor_tensor", dict(out=dsl, in0=dsl, in1=zsl, op=ALU.add)),
                             reads=[R["ZP"]], writes=[RD])
                        P.op("dve", ("tensor_tensor", dict(out=dsl, in0=dsl, in1=poolf[:, side, :, :], op=ALU.mult)),
                             reads=[R_const], writes=[RD])
                        P.op("dve", ("tensor_tensor", dict(out=dsl, in0=dsl, in1=zsl, op=ALU.subtract)),
                             reads=[R["ZP"]], writes=[RD])
                    P.op("dve", ("tensor_copy", dict(out=DB[:, :, :W], in_=D32[:, :, :W])),
                         reads=[RD], writes=[RYA[2], RYA[3]])
                    P.op("sp", ("dma_start", dict(out=XT[:, :, :W], in_=xview(xsrc)[:, :, ta:ta + W])),
                         reads=[R_xsrc], writes=[R["XT"]], dma=True)

                    def side_conv():
                        for c in range(2):
                            pb = 7
                            fns = [("matmul", dict(out=psb[pb][:, :W], lhsT=DIAG[:, c * 31 + k, :], rhs=AG[:, c, ab + k:ab + k + W],
                                                   start=(k == 0), stop=(k == 30))) for k in range(31)]
                            P.op("pe", fns, reads=[R["DIAG"], R["AG"]], writes=[Rps[pb]])
                            P.op("dve", ("tensor_scalar", dict(out=U32[:, c, :W], in0=psb[pb][:, :W], scalar1=vcol(l, "convb", c),
                                                               scalar2=None, op0=ALU.add)),
                                 reads=[Rps[pb], R_const], writes=[RU])

                    def side_pool_ln1():
                        for c in range(2):
                            pb = 7
                            P.op("pe", ("matmul", dict(out=psb[pb][:, :W], lhsT=pwbd[:, c, :], rhs=DB[:, c, :W], start=True, stop=True)),
                                 reads=[RYA[2], RYA[3], R_pwbd], writes=[Rps[pb]])
                            P.op("dve", ("tensor_scalar", dict(out=YPC[:, c, :W], in0=psb[pb][:, :W], scalar1=vcol(l, "pscale", c),
                                                               scalar2=None, op0=ALU.mult)),
                                 reads=[Rps[pb], R_const], writes=[R["YPC"]])
                        pm = 7
                        fns = [("matmul", dict(out=psb[pm][:, :W], lhsT=ones_f, rhs=U32[:, c, :W], start=(c == 0), stop=(c == 1)))
                               for c in range(2)]
                        P.op("pe", fns, reads=[RU, R_const], writes=[Rps[pm]])
                        for c in range(2):
                            P.op("dve", ("scalar_tensor_tensor", dict(out=U32[:, c, :W], in0=psb[pm][:, :W], scalar=-1.0 / 256.0,
                                                                      in1=U32[:, c, :W], op0=ALU.mult, op1=ALU.add)),
                                 reads=[Rps[pm], RU], writes=[RU])
                        P.op("dve", ("tensor_tensor", dict(out=D32[:, :, :W], in0=U32[:, :, :W], in1=U32[:, :, :W], op=ALU.mult)),
                             reads=[RU], writes=[RD])

                    def side_ln2():
                        pv = 7
                        fns = [("matmul", dict(out=psb[pv][:, :W], lhsT=ones_f, rhs=D32[:, c, :W], start=(c == 0), stop=(c == 1)))
                               for c in range(2)]
                        P.op("pe", fns, reads=[RD, R_const], writes=[Rps[pv]])
                        LT = D32[:, 0, :]
                        P.op("act", ("activation", dict(out=LT[:, :W], in_=psb[pv][:, :W], func=AF.Sqrt,
                                                        bias=epsc[:, 0:1], scale=1.0 / 256.0)),
                             reads=[Rps[pv], R_const], writes=[RD])
                        P.op("dve", ("reciprocal", dict(out=LT[:, :W], in_=LT[:, :W])), reads=[RD], writes=[RD])
                        for c in range(2):
                            P.op("dve", ("tensor_tensor", dict(out=U32[:, c, :W], in0=U32[:, c, :W], in1=LT[:, :W], op=ALU.mult)),
                                 reads=[RD, RU], writes=[RU])

                    def side_silu():
                        for c in range(2):
                            P.op("act", ("activation", dict(out=YPC[:, 2 + c, :W], in_=U32[:, c, :W], func=AF.Silu,
                                                            bias=vcol(l, "lnb", c), scale=vcol(l, "lng", c))),
                                 reads=[RU, R_const], writes=[R["YPC"]])

                    sides = {0: [side_conv], 1: [side_pool_ln1], 2: [side_ln2], 3: [side_silu]}

                    for h in range(4):
                        po = 3 + (h % 2)
                        psum_ = 5 + (h % 2)

                        def s_step(kt, h=h):
                            pss = kt % 3
                            kc0 = kbase + kt * 128
                            P.op("pe", [("matmul", dict(out=psb[pss][:, :W], lhsT=KN[:, h, kc0:kc0 + 128],
                                                        rhs=QNP[:, h, :W], start=True, stop=False)),
                                        ("matmul", dict(out=psb[pss][:, :W], lhsT=KR[:, kc0:kc0 + 128],
                                                        rhs=QRP[:, h, :W], start=False, stop=True))],
                                 reads=[R["KN"], R["KR"], RQN[h], RQR[h]], writes=[Rps[pss]])
                            pt = PT[kt % 4]
                            P.op("act", ("activation", dict(out=pt[:, :W], in_=psb[pss][:, :W], func=AF.Exp, scale=ATT_SCALE)),
                                 reads=[Rps[pss]], writes=[RPT[kt % 4]])

                        def pv_step(kt, h=h, po=po, psum_=psum_):
                            ktg = (kbase + kt * 128) // 128
                            pt = PT[kt % 4]
                            P.op("pe", [("matmul", dict(out=psb[po][:, :W], lhsT=VV[:, ktg, h * 128:(h + 1) * 128], rhs=pt[:, :W],
                                                        start=(kt == 0), stop=(kt == nkt - 1))),
                                        ("matmul", dict(out=psb[psum_][:, :W], lhsT=ones_b, rhs=pt[:, :W],
                                                        start=(kt == 0), stop=(kt == nkt - 1)))],
                                 reads=[R["VV"], RPT[kt % 4], R_const], writes=[Rps[po], Rps[psum_]])

                        LA = 2
                        for kt in range(nkt + LA):
                            if kt < nkt:
                                s_step(kt)
                            if kt >= LA:
                                pv_step(kt - LA)
                        P.op("dve", ("reciprocal", dict(out=RT[:, :W], in_=psb[psum_][:, :W])),
                             reads=[Rps[psum_]], writes=[R["RT"]])
                        P.op("dve", ("tensor_tensor", dict(out=YA[:, h, :W], in0=psb[po][:, :W], in1=RT[:, :W], op=ALU.mult)),
                             reads=[Rps[po], R["RT"]], writes=[RYA[h]])
                        for f in sides[h]:
                            f()
                    rot[0] = 0
                    if ti + 1 < len(tilesE):
                        qproj(*tilesE[ti + 1])
                    ycat = [YPC[:, 0, :], YPC[:, 1, :], YA[:, 0, :], YA[:, 1, :], YA[:, 2, :], YA[:, 3, :], YPC[:, 2, :], YPC[:, 3, :]]
                    for m in range(8):
                        pb = 7 if m % 2 == 0 else 0
                        kord = [0, 1, 6, 7, 2, 3, 4, 5]
                        fns = [("matmul", dict(out=psb[pb][:, :W], lhsT=WOUT[:, kc_, m * 128:(m + 1) * 128],
                                               rhs=ycat[kc_][:, :W], start=(ii == 0), stop=(ii == 7))) for ii, kc_ in enumerate(kord)]
                        P.op("pe", fns, reads=[R["WOUT"], R["YPC"]] + RYA, writes=[Rps[pb]])
                        P.op("dve", ("scalar_tensor_tensor", dict(
                            out=XT[:, m, :W], in0=psb[pb][:, :W], scalar=MOD[:, l, 16 + m, g:g + 1], in1=XT[:, m, :W],
                            op0=ALU.mult, op1=ALU.add)),
                            reads=[Rps[pb], R["XT"], Rm()], writes=[R["XT"]])
                    P.op("sp", ("dma_start", dict(out=xview(xdst)[:, :, ta:ta + W], in_=XT[:, :, :W])),
                         reads=[R["XT"]], writes=[R_xdst], dma=True)
            P.barrier()

        def ffn_phase(l, xsrc, xdst, R_xsrc, R_xdst, nxt=None):
            WMAX = 412
            AR.off = reg0
            GV = AR.alloc([NJ, 930], BF16)
            WDN = [AR.alloc([NJ, 128], BF16) for _ in range(3)]
            ACC = [AR.alloc([2, WMAX], F32) for _ in range(4)]
            SG = [AR.alloc([WMAX], F32) for _ in range(4)]
            XM = [AR.alloc([930], F32) for _ in range(3)]
            if MODS_INTERLEAVE and l + 1 < depth:
                mstg = [AR.alloc([8, 512], BF16) for _ in range(2)]
            assert AR.off <= FFN_HI, AR.off
            AR.off = FFN_HI
            WUP = [AR.alloc([8, 256], BF16) for _ in range(4)]
            XH = AR.alloc([8, WMAX], F32)
            SQH = AR.alloc([8, WMAX], BF16)
            H2 = AR.alloc([3 * 8, WMAX], BF16)
            RTH = AR.alloc([WMAX], F32)
            mt = None
            if MODS_INTERLEAVE and l + 1 < depth:
                mrow = [AR.alloc([512], F32) for _ in range(2)]
                reserved.add(7)
                mt = mod_tasks(l + 1, mstg, [Res("ms0"), Res("ms1")], mrow, [Res("mr0"), Res("mr1")], 7)
            R = {n: Res(n) for n in ("GV", "XH", "SQH", "RTH")}
            RH2 = [Res(f"H2{i}") for i in range(3)]
            RGV = [Res(f"GV{j}") for j in range(NJ)]
            RWUP = [Res(f"WUP{i}") for i in range(4)]
            RWDN = [Res(f"WDN{i}") for i in range(3)]
            RACC = [[Res(f"ACC{i}g"), Res(f"ACC{i}v")] for i in range(4)]
            RSG = [Res(f"SG{i}") for i in range(4)]
            RXM = [Res(f"XM{i}") for i in range(3)]

            def subt(t0, L, a, b, g):
                return dict(t0=t0, L=L, a=a, b=b, g=g, hl=1 if a > 0 else 0, hr=1 if b < L else 0)

            sb = [0, 410, 820, 1230, 1640, 2048]
            ssub = [subt(0, LS, sb[i], sb[i + 1], 0) for i in range(5)]
            psub = [subt(LS, LP, 0, LP, 1), subt(LS + LP, LP, 0, LP, 1)]
            supers = [ssub[0:2], ssub[2:4], [ssub[4]] + psub]
            cnt_acc = [0]
            pend = []
            cnt_xm = [0]
            def load_wup(j):
                P.op("pool", ("dma_start", dict(out=WUP[j % 4], in_=wup_d[l, j], max_dma_last_dim=4096)),
                     writes=[RWUP[j % 4]], dma=True)

            def load_wdn(m):
                P.op("pool", ("dma_start", dict(out=WDN[m % 3], in_=wdn_d[l, m], max_dma_last_dim=4096)),
                     writes=[RWDN[m % 3]], dma=True)

            def h2_parts(ST):
                parts = []
                gc = 0
                for i, t in enumerate(ST):
                    t["gc"] = gc
                    gc += t["b"] - t["a"]
                    Wh = t["b"] - t["a"] + t["hl"] + t["hr"]
                    ca = t["t0"] + t["a"] - t["hl"]
                    g = t["g"]
                    h2v = H2[:, i * 8:(i + 1) * 8, :]

                    def pa(Wh=Wh, ca=ca):
                        P.op("sp", ("dma_start", dict(out=XH[:, :, :Wh], in_=xview(xsrc)[:, :, ca:ca + Wh])),
                             reads=[R_xsrc], writes=[R["XH"]], dma=True)
                        emit_norm_a(XH, R["XH"], Wh, SQH, R["SQH"])

                    def pb_(Wh=Wh, g=g, h2v=h2v, i=i):
                        emit_norm_b(XH, R["XH"], Wh,
                                    lambda k: AMOD[:, l * 2 + 1, k, g:g + 1], lambda k: MOD[:, l, 24 + k, g:g + 1],
                                    h2v, RH2[i], SQH, R["SQH"], RTH, R["RTH"])
                    parts.append((pa, pb_))
                return parts

            def emit_h2(ST):
                for pa, pb_ in h2_parts(ST):
                    pa()
                    pb_()

            for jj in range(3):
                load_wup(jj)
            emit_h2(supers[0])
            yield
            for sti, ST in enumerate(supers):
                if sti == 0 and mt is not None:
                    mt[0][0]()
                    mt[0][1]()
                for j in range(NJ):
                    sl = j % 4
                    if j + 3 < NJ:
                        load_wup(j + 3)
                    if sti == 0 and mt is not None:
                        mb = (j - 1) // 2 if j % 2 == 1 else None
                        if j == NJ - 2:
                            mb = 10
                        if j == NJ - 1:
                            mb = 11
                        if mb is not None and mb < 12:
                            mt[1][mb]()
                            if mb + 2 < 12:
                                mt[0][mb + 2]()
                            if mb >= 1:
                                mt[2][mb - 1]()
                    if j == NJ - 3:
                        load_wdn(0)
                    if j == NJ - 2:
                        load_wdn(1)
                    for i, t in enumerate(ST):
                        W = t["b"] - t["a"]
                        hl, hr = t["hl"], t["hr"]
                        Wh = W + hl + hr
                        h2v = H2[:, i * 8:(i + 1) * 8, :]
                        ai = cnt_acc[0] % 4
                        cnt_acc[0] += 1
                        acc = ACC[ai]
                        pbs = []
                        for half in range(2):
                            pb = nb()
                            pbs.append(pb)
                            fns = [("matmul", dict(out=psb[pb][:, :Wh], lhsT=WUP[sl][:, k, half * 128:(half + 1) * 128], rhs=h2v[:, k, :Wh],
                                start=(k == 0), stop=(k == 7))) for k in range(8)]
                            P.op("pe", fns, reads=[RWUP[sl], RH2[i]], writes=[Rps[pb]])
                            ch = half * NJ + j
                            P.op("act", ("activation", dict(
                                out=acc[:, half, :W], in_=psb[pb][:, hl:hl + W], func=AF.Identity,
                                bias=vcol(l, "fcb", ch), scale=vcol(l, "fcw", 1 * 44 + ch))),
                                reads=[Rps[pb], R_const], writes=[RACC[ai][half]])
                            lo = 0 if hl else 1
                            P.op("dve", ("scalar_tensor_tensor", dict(
                                out=acc[:, half, lo:W], in0=psb[pb][:, hl - 1 + lo:hl - 1 + W], scalar=vcol(l, "fcw", 0 * 44 + ch),
                                in1=acc[:, half, lo:W], op0=ALU.mult, op1=ALU.add)),
                                reads=[Rps[pb], R_const], writes=[RACC[ai][half]])
                            hi = W if hr else W - 1
                            P.op("dve", ("scalar_tensor_tensor", dict(
                                out=acc[:, half, 0:hi], in0=psb[pb][:, hl + 1:hl + 1 + hi], scalar=vcol(l, "fcw", 2 * 44 + ch),
                                in1=acc[:, half, 0:hi], op0=ALU.mult, op1=ALU.add)),
                                reads=[Rps[pb], R_const], writes=[RACC[ai][half]])
                        gcol = t["gc"]

                        def fin(ai=ai, W=W, acc=acc, j=j, gcol=gcol):
                            P.op("act", ("activation", dict(out=SG[ai][:, :W], in_=acc[:, 0, :W], func=AF.Silu)),
                                 reads=[RACC[ai][0]], writes=[RSG[ai]])
                            P.op("pool", ("tensor_tensor", dict(
                                out=GV[:, j, gcol:gcol + W], in0=SG[ai][:, :W], in1=acc[:, 1, :W], op=ALU.mult)),
                                reads=[RSG[ai], RACC[ai][1]], writes=[RGV[j]])

                        pend.append(fin)
                        if len(pend) > 1:
                            pend.pop(0)()
                while pend:
                    pend.pop(0)()
                if sti == 0 and mt is not None:
                    mt[2][11]()
                    mt[3]()
                    reserved.discard(7)
                nparts = []
                if sti + 1 == len(supers) and nxt is not None:
                    guard = [R["XH"], R["SQH"], R["RTH"]] + RH2
                    P.op("pool", ("dma_start", dict(out=nxt["WIN"], in_=win_d[l + 1], max_dma_last_dim=4096)),
                         writes=[nxt["R_WIN"]] + guard, dma=True)
                    P.op("pool", ("dma_start", dict(out=nxt["WUKV"], in_=wukv_d[l + 1], max_dma_last_dim=4096)),
                         writes=[nxt["R_WUKV"]] + guard, dma=True)
                if sti + 1 < len(supers):
                    for jj in range(3):
                        load_wup(jj)
                    nparts = h2_parts(supers[sti + 1])
                    nparts[0][0]()
                for m in range(8):
                    sl = m % 3
                    if m + 2 < 8:
                        load_wdn(m + 2)
                    ca0 = ST[0]["t0"] + ST[0]["a"]
                    Wtot = sum(t["b"] - t["a"] for t in ST)
                    for i, t in enumerate(ST):
                        assert t["t0"] + t["a"] - ca0 == t["gc"]
                    xi = cnt_xm[0] % 3
                    cnt_xm[0] += 1
                    P.op("sp", ("dma_start", dict(out=XM[xi][:, :Wtot], in_=xsrc[m, :, ca0:ca0 + Wtot])),
                         reads=[R_xsrc], writes=[RXM[xi]], dma=True)
                    for i, t in enumerate(ST):
                        W = t["b"] - t["a"]
                        gcol = t["gc"]
                        g = t["g"]
                        pb = nb()
                        fns = [("matmul", dict(out=psb[pb][:, :W], lhsT=WDN[sl][:, j, :], rhs=GV[:, j, gcol:gcol + W],
                            start=(j == 0), stop=(j == NJ - 1))) for j in range(NJ)]
                        JS = NJ - 4
                        P.op("pe", fns[:JS], reads=[RWDN[sl]] + RGV[:JS], writes=[Rps[pb]])
                        P.op("pe", fns[JS:], reads=[RWDN[sl]] + RGV[JS:], writes=[Rps[pb]])
                        P.op("dve", ("scalar_tensor_tensor", dict(
                            out=XM[xi][:, gcol:gcol + W], in0=psb[pb][:, :W], scalar=MOD[:, l, 40 + m, g:g + 1],
                            in1=XM[xi][:, gcol:gcol + W], op0=ALU.mult, op1=ALU.add)),
                            reads=[Rps[pb], RXM[xi], Rm()], writes=[RXM[xi]])
                    P.op("sp", ("dma_start", dict(out=xdst[m, :, ca0:ca0 + Wtot], in_=XM[xi][:, :Wtot])),
                         reads=[RXM[xi]], writes=[R_xdst], dma=True)
                    if nparts and m % 2 == 1:
                        pi = m // 2
                        if pi < len(nparts):
                            nparts[pi][1]()
                            if pi + 1 < len(nparts):
                                nparts[pi + 1][0]()
            P.barrier()

        def final_norm(xsrc, R_xsrc):
            AR.off = reg0
            XT = AR.alloc([8, 512], F32)
            SQ = AR.alloc([8, 512], BF16)
            YO = AR.alloc([8, 512], F32)
            RT = AR.alloc([512], F32)
            RX, RS, RY, RR = Res("fx"), Res("fs"), Res("fy"), Res("fr")
            for a in range(0, NTOK, 512):
                W = 512
                P.op("sp", ("dma_start", dict(out=XT[:, :, :W], in_=xview(xsrc)[:, :, a:a + W])),
                     reads=[R_xsrc], writes=[RX], dma=True)
                P.op("act", ("activation", dict(out=SQ[:, :, :W], in_=XT[:, :, :W], func=AF.Square)), reads=[RX], writes=[RS])
                pb = nb()
                fns = [("matmul", dict(out=psb[pb][:, :W], lhsT=ones_b, rhs=SQ[:, k, :W], start=(k == 0), stop=(k == 7)))
                       for k in range(8)]
                P.op("pe", fns, reads=[RS, R_const], writes=[Rps[pb]])
                emit_rstd(pb, W, float(D), RT, RR)
                for k in range(8):
                    P.op("dve", ("scalar_tensor_tensor", dict(
                        out=YO[:, k, :W], in0=XT[:, k, :W], scalar=vecs[:, DEPTH * VPL + k:DEPTH * VPL + k + 1], in1=psb[pb][:, :W],
                        op0=ALU.mult, op1=ALU.mult)),
                        reads=[RX, Rps[pb], R_const], writes=[RY])
                o = P.op("sp", ("dma_start", dict(out=xview(y_d)[:, :, a:a + W], in_=YO[:, :, :W])), reads=[RY], dma=True)
                out_tokens.append(o.tok)

        R_xin, R_X0, R_X1 = Res("xin"), Res("X0"), Res("X1")
        AR.off = reg0
        for shp, dt_ in (([4, 2560], BF16), ([20, 512], BF16), ([2560], BF16), ([3, 2048], BF16), ([2, 2048 + 16], F32),
                         ([2, 2048 + 30], BF16), ([8, 1024], BF16), ([8, 512], F32), ([4, 512], F32), ([4, 512], BF16), ([512], F32)):
            AR.alloc(shp, dt_)
        smp_slot0 = AR.off
        smp_WIN = AR.alloc([8, INW], BF16)
        smp_WUKV = AR.alloc([2, 1024], BF16)
        pre_next = None
        for l in range(depth):
            cur_l[0] = l
            src, Rsrc = (xin, R_xin) if l == 0 else (X0, R_X0)
            mixer_pass(l, sample_seqs, src, X1, Rsrc, R_X1, pre=pre_next)
            pre_next = None
            if l + 1 < depth:
                pre_next = dict(off=smp_slot0, WIN=smp_WIN, WUKV=smp_WUKV, R_WIN=Res("WINp"), R_WUKV=Res("WUKVp"))
            fg = ffn_phase(l, X1, X0, R_X1, R_X0, nxt=pre_next)
            mixer_pass(l, prompt_seqs, src, X1, Rsrc, R_X1, hook=lambda: next(fg))
            for _ in fg:
                pass
        final_norm(X0, R_X0)
        P.emit(final_waits=out_tokens)
    return nc


def _host_layout(inp):
    f32 = np.float32
    D_ = D
    x_prompt = np.asarray(inp["x_prompt"], f32)
    x_sample = np.asarray(inp["x_sample"], f32)
    def fm(v):
        v = np.asarray(v, f32)
        return v.reshape(-1, 128).T

    vecs = np.zeros((128, NV), f32)
    for l in range(DEPTH):
        b = l * VPL
        vecs[:, b + VO["n1g"]:b + VO["n1g"] + 8] = fm(inp["norm1_g"][l])
        vecs[:, b + VO["n2g"]:b + VO["n2g"] + 8] = fm(inp["norm2_g"][l])
        vecs[:, b + VO["adab"]:b + VO["adab"] + 48] = fm(inp["ada_b"][l])
        vecs[:, b + VO["pscale"]:b + VO["pscale"] + 2] = fm(inp["pool_scale"][l])
        vecs[:, b + VO["qg"]:b + VO["qg"] + 3] = fm(inp["q_norm_g"][l])
        vecs[:, b + VO["kvg"]:b + VO["kvg"] + 2] = fm(inp["kv_norm_g"][l])
        cw = np.asarray(inp["conv_w"][l], f32)
        for k in range(31):
            vecs[:, b + VO["convw"] + 2 * k:b + VO["convw"] + 2 * k + 2] = fm(cw[k])
        vecs[:, b + VO["convb"]:b + VO["convb"] + 2] = fm(inp["conv_b"][l])
        vecs[:, b + VO["lng"]:b + VO["lng"] + 2] = fm(inp["conv_ln_g"][l])
        vecs[:, b + VO["lnb"]:b + VO["lnb"] + 2] = fm(inp["conv_ln_b"][l])
        fw = np.asarray(inp["ffn_conv_w"][l], f32)
        for k in range(3):
            vecs[:, b + VO["fcw"] + 44 * k:b + VO["fcw"] + 44 * (k + 1)] = fm(fw[k])
        vecs[:, b + VO["fcb"]:b + VO["fcb"] + 44] = fm(inp["ffn_conv_b"][l])
    vecs[:, DEPTH * VPL:DEPTH * VPL + 8] = fm(inp["final_g"])

    n = 16
    inv = (10000.0 ** (-np.arange(n, dtype=f32) / n)).astype(f32)
    row = np.repeat(np.arange(LS // 64, dtype=f32), 64)
    col = np.tile(np.arange(64, dtype=f32), LS // 64)
    ar = (row[:, None] * inv).astype(f32)
    ac = (col[:, None] * inv).astype(f32)
    cs = np.zeros((128, LS), f32)
    cr, sr, cc, sc = np.cos(ar).T, np.sin(ar).T, np.cos(ac).T, np.sin(ac).T
    cs[0:16], cs[16:32], cs[32:48], cs[48:64] = cr, cr, cc, cc
    cs[64:80], cs[80:96], cs[96:112], cs[112:128] = -sr, sr, -sc, sc
    swap = np.concatenate([np.arange(16, 32), np.arange(0, 16), np.arange(48, 64), np.arange(32, 48)])

    ident = np.eye(128, dtype=f32)
    wins = {(0, 0): 2, (0, 1): 4, (1, 0): 8, (1, 1): 16}
    poolf = np.ones((128, 2, 2, 8), f32)
    invw = np.zeros((128, 2), f32)
    for c in range(2):
        for hf in range(2):
            w = wins[(c, hf)]
            p = slice(hf * 64, hf * 64 + 64)
            invw[p, c] = 1.0 / w
            for t in range(8):
                cl = min(w, t + w // 2)
                poolf[p, 0, c, t] = w / cl
                tr = 8 - t
                cr_ = min(w, tr + w // 2)
                poolf[p, 1, c, t] = w / cr_
    pwbd = np.zeros((DEPTH, 128, 2, 128), f32)
    pw = np.asarray(inp["pool_w"], f32)
    for l in range(DEPTH):
        for c in range(2):
            for hf in range(2):
                gidx = c * 2 + hf
                pwbd[l, hf * 64:hf * 64 + 64, c, hf * 64:hf * 64 + 64] = pw[l, gidx]

    def kmaj(w, kch):
        return np.ascontiguousarray(w.reshape(kch, 128, -1).transpose(1, 0, 2))

    ada = np.zeros((DEPTH, 12, 128, 8, 512), f32)
    win = np.zeros((DEPTH, 128, 8, INW), f32)
    wuq = np.zeros((DEPTH, 128, 3, 1024), f32)
    wukv = np.zeros((DEPTH, 128, 2, 1024), f32)
    wout = np.zeros((DEPTH, 128, 8, 1024), f32)
    wup = np.zeros((DEPTH, NJ, 128, 8, 256), f32)
    wdn = np.zeros((DEPTH, 8, 128, NJ, 128), f32)
    for l in range(DEPTH):
        aw = kmaj(np.asarray(inp["ada_w"][l], f32), 8)
        for blk in range(12):
            ada[l, blk] = aw[:, :, blk * 512:(blk + 1) * 512]
        wi = np.asarray(inp["w_in"][l], f32)
        wi = np.concatenate([wi, wi[:, 896 + swap]], axis=1)
        win[l] = kmaj(wi, 8)
        wq = np.asarray(inp["w_uq"][l], f32).reshape(384, 4, 192)
        wq_n = wq[:, :, :128].reshape(384, 512)
        wq_r = wq[:, :, 128:].reshape(384, 256)
        wq_s = wq[:, :, 128:][:, :, swap].reshape(384, 256)
        wuq[l] = kmaj(np.concatenate([wq_n, wq_r, wq_s], axis=1), 3)
        wk = np.asarray(inp["w_ukv"][l], f32).reshape(256, 4, 256)
        wukv[l] = kmaj(np.concatenate([wk[:, :, :128].reshape(256, 512), wk[:, :, 128:].reshape(256, 512)], axis=1), 2)
        wout[l] = kmaj(np.asarray(inp["w_out"][l], f32), 8)
        wu = kmaj(np.asarray(inp["w_up"][l], f32), 8)
        for j in range(NJ):
            wup[l, j, :, :, :128] = wu[:, :, j * 128:(j + 1) * 128]
            wup[l, j, :, :, 128:] = wu[:, :, DFF + j * 128:DFF + (j + 1) * 128]
        wd = kmaj(np.asarray(inp["w_down"][l], f32), NJ)
        for m in range(8):
            wdn[l, m] = wd[:, :, m * 128:(m + 1) * 128]

    cdiag = np.zeros((DEPTH, 128, 62, 128), f32)
    pidx = np.arange(128)
    for l in range(DEPTH):
        cw = np.asarray(inp["conv_w"][l], f32)
        for c in range(2):
            for k in range(31):
                cdiag[l, pidx, c * 31 + k, pidx] = cw[k, c * 128:(c + 1) * 128]
    shared = dict(cdiag=cdiag, vecs=vecs, cs=cs, ident=ident, poolf=poolf, invw=invw, pwbd=pwbd, ada=ada, win=win, wuq=wuq,
                  wukv=wukv, wout=wout, wup=wup, wdn=wdn)
    maps = []
    cckv = np.asarray(inp["cache_ckv"], f32)
    ckr = np.asarray(inp["cache_krope"], f32)
    cvec = np.asarray(inp["c"], f32)
    cctx = np.asarray(inp["c_ctx"], f32)
    for c in range(NCORES):
        X = np.concatenate([x_sample[c], x_prompt[2 * c], x_prompt[2 * c + 1]], axis=0)
        xin = np.ascontiguousarray(X.T.reshape(8, 128, NTOK))
        craw = np.stack([fm(cvec[c]), fm(cctx)], axis=-1)
        ck = np.ascontiguousarray(cckv[c].transpose(0, 2, 1).reshape(DEPTH, 2, 128, PAST).transpose(0, 2, 1, 3))
        kr = np.ascontiguousarray(ckr[c].transpose(0, 2, 1))
        m = dict(shared)
        m.update(xin=xin, craw=np.ascontiguousarray(craw), cckv=ck, ckr=kr)
        maps.append(m)
    return maps


_NC_CACHE = {}


def kernel(**inputs):
    maps = _host_layout(inputs)
    if "nc" not in _NC_CACHE:
        _NC_CACHE["nc"] = build()
    nc = _NC_CACHE["nc"]
    res = run_bass_kernel_spmd(nc, maps, core_ids=list(range(NCORES)))
    y_prompt = np.zeros((16, LP, D), np.float32)
    y_sample = np.zeros((8, LS, D), np.float32)
    s_ckv = np.zeros((16, DEPTH, LP, 256), np.float32)
    s_kr = np.zeros((16, DEPTH, LP, 64), np.float32)
    for c in range(NCORES):
        r = res.results[c]
        Y = np.asarray(r["y"]).reshape(D, NTOK).T
        y_sample[c] = Y[:LS]
        y_prompt[2 * c] = Y[LS:LS + LP]
        y_prompt[2 * c + 1] = Y[LS + LP:]
        ck = np.asarray(r["sckv"])
        kr = np.asarray(r["skr"])
        for s in range(2):
            s_ckv[2 * c + s] = ck[:, s].reshape(DEPTH, 256, LP).transpose(0, 2, 1)
            s_kr[2 * c + s] = kr[:, s].transpose(0, 2, 1)
    return (y_prompt, y_sample, s_ckv, s_kr)
```
